# Optimizing a Trainium2 kernel written in Bass

```python
import jax, jax.numpy as jnp
from jax import lax
import numpy as np

D_MODEL = 1024
BATCH = 16
SEQ = 256
DEPTH = 2
DEC_BATCH = 8
DEC_SEQ = 2048
PAST_LEN = 256

GRID_W = 64
N_CONV_LAYERS = (DEPTH + 1) // 2
N_ATTN_LAYERS = DEPTH // 2
SC_WIDTH = D_MODEL // 2
SC_KERNEL = 3
CF_WIDTH = D_MODEL // 2
CF_KERNEL = 31
GQA_HEADS = 8
GQA_KV_HEADS = 2
GQA_HEAD_DIM = 64
MLA_HEADS = 8
MLA_Q_LORA = 384
MLA_KV_LORA = 256
MLA_NOPE = 64
MLA_ROPE = 32
MLA_V = 64
FFN_HIDDEN = ((8 * D_MODEL // 3 + 255) // 256) * 256
ROPE_THETA = 10000.0
NORM_EPS = 1e-6
Q_BLOCK = 128
N_MOD = 6
CONV_IN = 3 * SC_WIDTH + 2 * CF_WIDTH
CONV_MIX = SC_WIDTH + CF_WIDTH
GQA_Q = GQA_HEADS * GQA_HEAD_DIM
GQA_KV = GQA_KV_HEADS * GQA_HEAD_DIM
ATTN_IN = GQA_Q + 2 * GQA_KV + MLA_Q_LORA + MLA_KV_LORA + MLA_ROPE
ATTN_MIX = GQA_Q + MLA_HEADS * MLA_V

kernel_name = "hybrid_diffusion_prefix_step"


def _rmsnorm(x, g):
    xf = x.astype(jnp.float32)
    y = xf * lax.rsqrt(jnp.mean(xf * xf, axis=-1, keepdims=True) + NORM_EPS)
    return (y * g.astype(jnp.float32)).astype(x.dtype)


def _layernorm(x, g, b):
    xf = x.astype(jnp.float32)
    mu = jnp.mean(xf, axis=-1, keepdims=True)
    var = jnp.mean(jnp.square(xf - mu), axis=-1, keepdims=True)
    y = (xf - mu) * lax.rsqrt(var + NORM_EPS)
    return (y * g.astype(jnp.float32) + b.astype(jnp.float32)).astype(x.dtype)


def _adaln(cvec, w, b):
    m = jax.nn.silu(cvec) @ w + b
    return m.reshape(cvec.shape[0], N_MOD, D_MODEL)


def _modulate(h, shift, scale):
    return h * (1.0 + scale[:, None, :]) + shift[:, None, :]


def _rope_1d(x, pos):
    d = x.shape[-1]
    inv = ROPE_THETA ** (-jnp.arange(0, d, 2, dtype=jnp.float32) / d)
    ang = pos[:, None] * inv[None, :]
    cos = jnp.cos(ang)[:, None, :]
    sin = jnp.sin(ang)[:, None, :]
    xf = x.astype(jnp.float32)
    x1, x2 = xf[..., : d // 2], xf[..., d // 2:]
    return jnp.concatenate([x1 * cos - x2 * sin, x2 * cos + x1 * sin], axis=-1).astype(x.dtype)


def _rope_2d(x, pos):
    row, col = pos
    half = x.shape[-1] // 2
    return jnp.concatenate([_rope_1d(x[..., :half], row), _rope_1d(x[..., half:], col)], axis=-1)


def _depthwise_conv(x, w):
    width = w.shape[0]
    pad = (width - 1) // 2
    return lax.conv_general_dilated(x, w[:, None, :].astype(x.dtype), window_strides=(1,),
                                    padding=[(pad, width - 1 - pad)],
                                    dimension_numbers=("NWC", "WIO", "NWC"),
                                    feature_group_count=x.shape[-1])


def _conv_mixers(h, w_in, sc_w, cf_b_in, cf_dw_w, cf_dw_b, cf_ln_g, cf_ln_b, w_out, b_out):
    u = h @ w_in
    gate_b = u[..., :SC_WIDTH]
    gate_c = u[..., SC_WIDTH:2 * SC_WIDTH]
    xa = u[..., 2 * SC_WIDTH:3 * SC_WIDTH]
    ub = u[..., 3 * SC_WIDTH:] + cf_b_in
    ya = gate_b * _depthwise_conv(gate_c * xa, sc_w)
    z = ub[..., :CF_WIDTH] * jax.nn.sigmoid(ub[..., CF_WIDTH:])
    z = _depthwise_conv(z, cf_dw_w) + cf_dw_b
    z = jax.nn.silu(_layernorm(z, cf_ln_g, cf_ln_b))
    return jnp.concatenate([ya, z], axis=-1) @ w_out + b_out


def _attn_project(h, w_in, q_norm, k_norm, q_a_norm, w_q_b, kv_a_norm, pos):
    bsz, length, _ = h.shape
    u = h @ w_in
    o1, o2, o3 = GQA_Q, GQA_Q + GQA_KV, GQA_Q + 2 * GQA_KV
    o4 = o3 + MLA_Q_LORA
    q = _rmsnorm(u[..., :o1].reshape(bsz, length, GQA_HEADS, GQA_HEAD_DIM), q_norm)
    k = _rmsnorm(u[..., o1:o2].reshape(bsz, length, GQA_KV_HEADS, GQA_HEAD_DIM), k_norm)
    v = u[..., o2:o3].reshape(bsz, length, GQA_KV_HEADS, GQA_HEAD_DIM)
    mq = (_rmsnorm(u[..., o3:o4], q_a_norm) @ w_q_b).reshape(bsz, length, MLA_HEADS, MLA_NOPE + MLA_ROPE)
    kva = u[..., o4:]
    ckv = _rmsnorm(kva[..., :MLA_KV_LORA], kv_a_norm)
    kr = kva[..., MLA_KV_LORA:]
    if pos is not None:
        q = _rope_2d(q, pos)
        k = _rope_2d(k, pos)
        mq = jnp.concatenate([mq[..., :MLA_NOPE], _rope_2d(mq[..., MLA_NOPE:], pos)], axis=-1)
        kr = _rope_2d(kr[:, :, None, :], pos)[:, :, 0, :]
    return q, k, v, mq, ckv, kr


def _mla_expand(ckv, kr, w_kv_b):
    bsz, t, _ = ckv.shape
    kv = (ckv @ w_kv_b).reshape(bsz, t, MLA_HEADS, MLA_NOPE + MLA_V)
    k_rope = jnp.broadcast_to(kr[:, :, None, :], (bsz, t, MLA_HEADS, MLA_ROPE))
    k = jnp.concatenate([kv[..., :MLA_NOPE], k_rope], axis=-1)
    return k, kv[..., MLA_NOPE:]


def _attend(q, k, v, scale):
    bsz, s, hk, g, dk = q.shape
    nb = s // Q_BLOCK
    qb = q.reshape(bsz, nb, Q_BLOCK, hk, g, dk).transpose(1, 0, 2, 3, 4, 5)

    def one_block(qblk):
        sc = jnp.einsum("bqhgd,bthd->bhgqt", qblk, k, preferred_element_type=jnp.float32) * scale
        p = jax.nn.softmax(sc, axis=-1).astype(v.dtype)
        return jnp.einsum("bhgqt,bthd->bqhgd", p, v)

    o = lax.map(one_block, qb)
    return o.transpose(1, 0, 2, 3, 4, 5).reshape(bsz, s, hk * g * v.shape[-1])


def _attn_merge(qc, kc, vc, qm, km, vm, w_out):
    bsz, s = qc.shape[0], qc.shape[1]
    qc5 = qc.reshape(bsz, s, GQA_KV_HEADS, GQA_HEADS // GQA_KV_HEADS, GQA_HEAD_DIM)
    oc = _attend(qc5, kc, vc, GQA_HEAD_DIM ** -0.5)
    om = _attend(qm[:, :, :, None, :], km, vm, (MLA_NOPE + MLA_ROPE) ** -0.5)
    return jnp.concatenate([oc, om], axis=-1) @ w_out


def _swiglu(h, wg, wu, wd):
    return (jax.nn.silu(h @ wg) * (h @ wu)) @ wd


def setup_inputs(seed: int = 0) -> dict:
    key = jax.random.key(seed)
    ks = iter(jax.random.split(key, 40))
    f32 = jnp.float32

    def nrm(shape, scale=1.0):
        return jax.random.normal(next(ks), shape, f32) * scale

    def gain(shape):
        return 1.0 + nrm(shape, 0.05)

    na, nc = N_ATTN_LAYERS, N_CONV_LAYERS
    return {
        "x_prompt": nrm((BATCH, SEQ, D_MODEL)),
        "x_sample": nrm((DEC_BATCH, DEC_SEQ, D_MODEL)),
        "cache_gqa_k": nrm((DEC_BATCH, na, PAST_LEN, GQA_KV_HEADS, GQA_HEAD_DIM)),
        "cache_gqa_v": nrm((DEC_BATCH, na, PAST_LEN, GQA_KV_HEADS, GQA_HEAD_DIM)),
        "cache_mla_ckv": nrm((DEC_BATCH, na, PAST_LEN, MLA_KV_LORA)),
        "cache_mla_krope": nrm((DEC_BATCH, na, PAST_LEN, MLA_ROPE)),
        "c": nrm((DEC_BATCH, D_MODEL)),
        "c_ctx": nrm((D_MODEL,)),
        "ada_w": nrm((DEPTH, D_MODEL, N_MOD * D_MODEL), 0.5 * D_MODEL ** -0.5),
        "ada_b": nrm((DEPTH, N_MOD * D_MODEL), 0.02),
        "norm_pre": gain((DEPTH, 2, D_MODEL)),
        "norm_post": gain((DEPTH, 2, D_MODEL)),
        "conv_w_in": nrm((nc, D_MODEL, CONV_IN), D_MODEL ** -0.5),
        "conv_sc_w": nrm((nc, SC_KERNEL, SC_WIDTH), SC_KERNEL ** -0.5),
        "conv_cf_b_in": nrm((nc, 2 * CF_WIDTH), 0.02),
        "conv_cf_dw_w": nrm((nc, CF_KERNEL, CF_WIDTH), CF_KERNEL ** -0.5),
        "conv_cf_dw_b": nrm((nc, CF_WIDTH), 0.02),
        "conv_cf_ln_g": gain((nc, CF_WIDTH)),
        "conv_cf_ln_b": nrm((nc, CF_WIDTH), 0.02),
        "conv_w_out": nrm((nc, CONV_MIX, D_MODEL), CONV_MIX ** -0.5),
        "conv_b_out": nrm((nc, D_MODEL), 0.02),
        "attn_w_in": nrm((na, D_MODEL, ATTN_IN), D_MODEL ** -0.5),
        "attn_q_norm": gain((na, GQA_HEAD_DIM)),
        "attn_k_norm": gain((na, GQA_HEAD_DIM)),
        "attn_q_a_norm": gain((na, MLA_Q_LORA)),
        "attn_w_q_b": nrm((na, MLA_Q_LORA, MLA_HEADS * (MLA_NOPE + MLA_ROPE)), MLA_Q_LORA ** -0.5),
        "attn_kv_a_norm": gain((na, MLA_KV_LORA)),
        "attn_w_kv_b": nrm((na, MLA_KV_LORA, MLA_HEADS * (MLA_NOPE + MLA_V)), MLA_KV_LORA ** -0.5),
        "attn_w_out": nrm((na, ATTN_MIX, D_MODEL), ATTN_MIX ** -0.5),
        "ffn_w_gate": nrm((DEPTH, D_MODEL, FFN_HIDDEN), D_MODEL ** -0.5),
        "ffn_w_up": nrm((DEPTH, D_MODEL, FFN_HIDDEN), D_MODEL ** -0.5),
        "ffn_w_down": nrm((DEPTH, FFN_HIDDEN, D_MODEL), FFN_HIDDEN ** -0.5),
    }


def reference(x_prompt, x_sample, cache_gqa_k, cache_gqa_v, cache_mla_ckv, cache_mla_krope, c, c_ctx,
              ada_w, ada_b, norm_pre, norm_post,
              conv_w_in, conv_sc_w, conv_cf_b_in, conv_cf_dw_w, conv_cf_dw_b, conv_cf_ln_g, conv_cf_ln_b,
              conv_w_out, conv_b_out,
              attn_w_in, attn_q_norm, attn_k_norm, attn_q_a_norm, attn_w_q_b, attn_kv_a_norm, attn_w_kv_b,
              attn_w_out, ffn_w_gate, ffn_w_up, ffn_w_down):
    length = x_sample.shape[1]
    rows = length // GRID_W
    lat_pos = (jnp.repeat(jnp.arange(rows, dtype=jnp.float32), GRID_W),
               jnp.tile(jnp.arange(GRID_W, dtype=jnp.float32), rows))
    xp, xs = x_prompt, x_sample
    new_k, new_v, new_ckv, new_kr = [], [], [], []
    for l in range(DEPTH):
        mp = _adaln(c_ctx[None, :].astype(xp.dtype), ada_w[l], ada_b[l])
        ms = _adaln(c, ada_w[l], ada_b[l])
        hp = _modulate(_rmsnorm(xp, norm_pre[l, 0]), mp[:, 0], mp[:, 1])
        hs = _modulate(_rmsnorm(xs, norm_pre[l, 0]), ms[:, 0], ms[:, 1])
        j = l // 2
        if l % 2 == 0:
            cargs = (conv_w_in[j], conv_sc_w[j], conv_cf_b_in[j], conv_cf_dw_w[j], conv_cf_dw_b[j],
                     conv_cf_ln_g[j], conv_cf_ln_b[j], conv_w_out[j], conv_b_out[j])
            op = _conv_mixers(hp, *cargs)
            os_ = _conv_mixers(hs, *cargs)
        else:
            pargs = (attn_w_in[j], attn_q_norm[j], attn_k_norm[j], attn_q_a_norm[j], attn_w_q_b[j],
                     attn_kv_a_norm[j])
            qc, kc, vc, qm, ckv, kr = _attn_project(hp, *pargs, None)
            new_k.append(kc)
            new_v.append(vc)
            new_ckv.append(ckv)
            new_kr.append(kr)
            km, vm = _mla_expand(ckv, kr, attn_w_kv_b[j])
            op = _attn_merge(qc, kc, vc, qm, km, vm, attn_w_out[j])
            qcs, kcs, vcs, qms, ckvs, krs = _attn_project(hs, *pargs, lat_pos)
            kc_all = jnp.concatenate([cache_gqa_k[:, j].astype(kcs.dtype), kcs], axis=1)
            vc_all = jnp.concatenate([cache_gqa_v[:, j].astype(vcs.dtype), vcs], axis=1)
            ckv_all = jnp.concatenate([cache_mla_ckv[:, j].astype(ckvs.dtype), ckvs], axis=1)
            kr_all = jnp.concatenate([cache_mla_krope[:, j].astype(krs.dtype), krs], axis=1)
            kms, vms = _mla_expand(ckv_all, kr_all, attn_w_kv_b[j])
            os_ = _attn_merge(qcs, kc_all, vc_all, qms, kms, vms, attn_w_out[j])
        xp = xp + mp[:, 2][:, None, :] * _rmsnorm(op, norm_post[l, 0])
        xs = xs + ms[:, 2][:, None, :] * _rmsnorm(os_, norm_post[l, 0])
        fp = _modulate(_rmsnorm(xp, norm_pre[l, 1]), mp[:, 3], mp[:, 4])
        fs = _modulate(_rmsnorm(xs, norm_pre[l, 1]), ms[:, 3], ms[:, 4])
        xp = xp + mp[:, 5][:, None, :] * _rmsnorm(_swiglu(fp, ffn_w_gate[l], ffn_w_up[l], ffn_w_down[l]), norm_post[l, 1])
        xs = xs + ms[:, 5][:, None, :] * _rmsnorm(_swiglu(fs, ffn_w_gate[l], ffn_w_up[l], ffn_w_down[l]), norm_post[l, 1])
    return (xp, xs, jnp.stack(new_k, axis=1), jnp.stack(new_v, axis=1), jnp.stack(new_ckv, axis=1), jnp.stack(new_kr, axis=1))
```

```python
import numpy as np
import concourse.bass as bass
import concourse.mybir as mybir
from concourse.bass_utils import run_bass_kernel_spmd

F32 = mybir.dt.float32
BF16 = mybir.dt.bfloat16
AF = mybir.ActivationFunctionType
ALU = mybir.AluOpType

D = 1024
NS = 2048
NP_ = 512
NT = NS + NP_
FF = 2816
EPS = 1e-6
THETA = 10000.0
NK = 256 + NS + NP_
ARENA0 = 16512
ARENA1 = 229344


class Sem:
    def __init__(self, h, is_dma=False):
        self.h = h
        self.val = 0
        self.is_dma = is_dma


class Res:
    __slots__ = ("w", "r", "name")

    def __init__(self, name=""):
        self.w = None
        self.r = {}
        self.name = name


class Sched:
    ENG = ("pe", "act", "dve", "pool", "sp")

    def __init__(self, nc):
        self.nc = nc
        self.q = {k: [] for k in self.ENG}
        self.esem = {k: Sem(nc.alloc_semaphore("sem_" + k)) for k in self.ENG}
        self.dsems = []
        self.waited = {k: {} for k in self.ENG}
        self.all_res = []
        self.nops = 0

    def res(self, name=""):
        r = Res(name)
        self.all_res.append(r)
        return r

    def dsem(self, name):
        name = "%s_%d" % (name, len(self.dsems))
        s = Sem(self.nc.alloc_semaphore(name), True)
        self.dsems.append(s)
        return s

    def op(self, eng, fn, reads=(), writes=(), sem=None, inc=1):
        deps = {}
        own = self.esem[eng]

        def add(tok):
            s, v = tok
            if deps.get(s, 0) < v:
                deps[s] = v

        for R in reads:
            if R.w is not None:
                if not (R.w[0] is own and eng == "pe"):
                    add(R.w)
        for R in writes:
            if R.w is not None and not (R.w[0] is own and eng == "pe"):
                add(R.w)
            for s, v in R.r.items():
                if not (s is own and eng == "pe"):
                    add((s, v))
        waits = []
        wd = self.waited[eng]
        for s, v in deps.items():
            if s.is_dma:
                v = s.val
            if wd.get(s, 0) >= v:
                continue
            wd[s] = v
            waits.append((s, v))
        if sem is None:
            sem = own
        sem.val += inc
        tok = (sem, sem.val)
        for R in reads:
            if R.r.get(sem, 0) < sem.val:
                R.r[sem] = sem.val
        for R in writes:
            R.w = tok
            R.r = {}
        self.q[eng].append((waits, fn, sem, inc))
        self.nops += 1

    def barrier(self):
        sems = list(self.esem.values()) + self.dsems
        for eng in self.ENG:
            waits = []
            wd = self.waited[eng]
            for s in sems:
                if s.val > 0 and wd.get(s, 0) < s.val:
                    wd[s] = s.val
                    waits.append((s, s.val))
            self.q[eng].append((waits, None, None, 0))
        for r in self.all_res:
            r.w = None
            r.r = {}

    def emit(self):
        nc = self.nc
        handles = {"pe": "tensor", "act": "scalar", "dve": "vector", "pool": "gpsimd", "sp": "sync"}
        self.barrier()
        with nc.Block() as block:
            for k in self.ENG:
                def body(e, k=k):
                    for waits, fn, sem, inc in self.q[k]:
                        for s, v in waits:
                            e.wait_ge(s.h, v)
                        if fn is None:
                            continue
                        inst = fn(e)
                        inst.then_inc(sem.h, inc)
                getattr(block, handles[k])(body)


class Rot:
    def __init__(self, items):
        self.items = items
        self.i = 0

    def next(self):
        it = self.items[self.i % len(self.items)]
        self.i += 1
        return it


def _rope_consts():
    t = np.arange(NS)
    row = (t // 64).astype(np.float64)
    col = (t % 64).astype(np.float64)
    C64 = np.ones((128, NS), np.float64)
    S64 = np.zeros((128, NS), np.float64)
    P64 = np.zeros((128, 128), np.float32)
    for p in range(128):
        d = p % 64
        pos = row if d < 32 else col
        i = d % 16
        inv = THETA ** (-(2.0 * i) / 32.0)
        ang = pos * np.float64(np.float32(inv))
        first = (d % 32) < 16
        C64[p] = np.cos(ang)
        S64[p] = -np.sin(ang) if first else np.sin(ang)
        partner = p + 16 if first else p - 16
        P64[partner, p] = 1.0
    C96 = np.ones((96, NS), np.float64)
    S96 = np.zeros((96, NS), np.float64)
    P96 = np.zeros((96, 96), np.float32)
    for p in range(64, 96):
        d = p - 64
        pos = row if d < 16 else col
        i = d % 8
        inv = THETA ** (-(2.0 * i) / 16.0)
        ang = pos * np.float64(np.float32(inv))
        first = (d % 16) < 8
        C96[p] = np.cos(ang)
        S96[p] = -np.sin(ang) if first else np.sin(ang)
        partner = p + 8 if first else p - 8
        P96[partner, p] = 1.0
    return (C64.astype(np.float32), S64.astype(np.float32), P64,
            C96.astype(np.float32), S96.astype(np.float32), P96)


LAST_NAMES = {}
LAST_SCHED = [None]
DEBUG_SKIP = set()

def build_nc(debug=None, stop=None):
    nc = bass.Bass("TRN2", target_bir_lowering=False)
    S = Sched(nc)
    LAST_SCHED[0] = S

    def din(name, shape):
        return nc.dram_tensor(name, list(shape), F32, kind="ExternalInput").ap()

    def dout(name, shape):
        return nc.dram_tensor(name, list(shape), F32, kind="ExternalOutput").ap()

    xs_d = din("xs", [NS, D]); xp_d = din("xp", [NP_, D])
    ck_d = din("ck", [256, 128]); cv_d = din("cv", [256, 128])
    cckv_d = din("cckv", [256, 256]); ckr_d = din("ckr", [256, 32])
    crow_d = din("crow", [8, 128]); cctx_d = din("cctx", [8, 128])
    ada_w_d = din("ada_w", [2, D, 6 * D]); ada_b_d = din("ada_b", [2, 48, 128])
    npre_d = din("norm_pre", [32, 128]); npost_d = din("norm_post", [32, 128])
    cwin_d = din("conv_w_in", [D, 2560]); scw_d = din("conv_sc_w", [12, 128])
    cfbin_d = din("conv_cf_b_in", [8, 128]); dww_d = din("conv_cf_dw_w", [124, 128])
    dwb_d = din("conv_cf_dw_b", [4, 128]); lng_d = din("conv_cf_ln_g", [4, 128]); lnb_d = din("conv_cf_ln_b", [4, 128])
    cwout_d = din("conv_w_out", [D, D]); cbout_d = din("conv_b_out", [8, 128])
    awin_d = din("attn_w_in", [D, 1440]); qn_d = din("attn_q_norm", [1, 64]); kn_d = din("attn_k_norm", [1, 64])
    qan_d = din("attn_q_a_norm", [3, 128]); wqb_d = din("attn_w_q_b", [384, 768])
    kvan_d = din("attn_kv_a_norm", [2, 128]); wkvb_d = din("attn_w_kv_b", [256, 1024])
    awout_d = din("attn_w_out", [D, D])
    wg_d = din("ffn_w_gate", [2, D, FF]); wu_d = din("ffn_w_up", [2, D, FF]); wd_d = din("ffn_w_down", [2, FF, D])
    C64_d = din("C64", [128, NS]); S64_d = din("S64", [128, NS]); P64_d = din("P64", [128, 128])
    C96_d = din("C96", [96, NS]); S96_d = din("S96", [96, NS]); P96_d = din("P96", [96, 96])

    ys_d = dout("ys", [NS, D]); yp_d = dout("yp", [NP_, D])
    nk_d = dout("nk", [NP_, 128]); nv_d = dout("nv", [NP_, 128])
    nckv_d = dout("nckv", [NP_, 256]); nkr_d = dout("nkr", [NP_, 32])
    dbg_d = {}
    if debug:
        for name, shape in debug.items():
            dbg_d[name] = dout("dbg_" + name, shape)

    off = [ARENA0]
    sbn = [0]

    def sb(name, shape, dt, at=None):
        sbn[0] += 1
        name = "%s_%d" % (name, sbn[0])
        n = int(np.prod(shape[1:])) * (4 if dt == F32 else 2)
        n = (n + 63) // 64 * 64
        o = off[0] if at is None else at
        t = nc.alloc_sbuf_tensor_at(name, list(shape), dt, offset=o)
        LAST_NAMES[name.rsplit("_", 1)[0]] = t.name
        if at is None:
            off[0] += n
            assert off[0] <= ARENA1, (name, off[0])
        else:
            assert o + n <= ARENA1, (name, o + n)
        return t

    x_scr = nc.dram_tensor("x_scr", [128, 8, NT], F32).ap()
    h_scr = nc.dram_tensor("h_scr", [128, 8, NT], BF16).ap()
    if stop is not None:
        xdump_d = dout("xdump", [128, 8, NT])
        d_dump = S.dsem("d_dump")
        _emit = S.emit

        def emit_with_dump():
            S.barrier()
            for b5 in range(NT // 512 if "dump" not in DEBUG_SKIP else 0):
                xb, xb_r, d_xl, d_xs = XBLK.next()
                S.op("sp", lambda e, b5=b5, xb=xb: e.dma_start(out=xb[:], in_=x_scr[:, :, b5 * 512:(b5 + 1) * 512]),
                     reads=rx(b5 * 512, 512), writes=[xb_r], sem=d_xl, inc=16)
                S.op("sp", lambda e, b5=b5, xb=xb: e.dma_start(out=xdump_d[:, :, b5 * 512:(b5 + 1) * 512], in_=xb[:]),
                     reads=[xb_r], sem=d_dump, inc=16)
            _emit()
        S.emit = emit_with_dump
    RX = [S.res("x%d" % i) for i in range(NT // 256)]

    def rx(t0, n):
        return RX[t0 // 256:(t0 + n + 255) // 256]

    xblk = [sb("xblk%d" % i, [128, 8, 512], F32) for i in range(2)]
    XBLK = Rot([(xblk[i], S.res("xblk%d" % i), S.dsem("d_xl%d" % i), S.dsem("d_xs%d" % i)) for i in range(2)])

    ident_f = sb("ident_f", [128, 128], F32)
    ident_b = sb("ident_b", [128, 128], BF16)
    ones_b = sb("ones_b", [128, 128], BF16)
    blk64_b = sb("blk64_b", [128, 128], BF16)
    P64_sb = sb("P64_sb", [128, 128], F32)
    P96_sb = sb("P96_sb", [96, 96], F32)
    epsc = sb("epsc", [128, 1], F32)
    NCOL = 384
    colv = sb("colv", [128, NCOL], F32)
    cs_b = sb("cs_b", [128, 8, 2], BF16)
    mod = sb("mod", [128, 2, 48, 2], F32)
    Acoef = sb("Acoef", [128, 2, 2, 8, 2], F32)
    Gcoef = sb("Gcoef", [128, 2, 2, 8, 2], F32)
    R_const = S.res("const")
    R_mod = S.res("mod")

    tmpf = [sb("tmpf%d" % i, [128, 512], F32) for i in range(5)]
    TMP = Rot([(tmpf[i], S.res("tmpf%d" % i)) for i in range(5)])
    tmpb = [sb("tmpb%d" % i, [128, 512], BF16) for i in range(4)]
    TMPB = Rot([(tmpb[i], S.res("tmpb%d" % i)) for i in range(4)])
    sqb = [sb("sqb%d" % i, [128, 4, 512], BF16) for i in range(2)]
    SQB = Rot([(sqb[i], S.res("sqb%d" % i)) for i in range(2)])
    rsb = [sb("rs%d" % i, [128, 512], F32) for i in range(3)]
    RS = Rot([(rsb[i], S.res("rs%d" % i)) for i in range(3)])

    ps = [nc.alloc_psum_tensor("ps%d" % i, [128, 512], F32) for i in range(8)]
    PS_RES = [S.res("ps%d" % i) for i in range(8)]
    PS = Rot([(ps[i], PS_RES[i]) for i in range(6)])
    PSA = Rot([(ps[i], PS_RES[i]) for i in range(6, 8)])
    PSN = Rot([(ps[6], PS_RES[6])])


    def mm(out, pairs, reads, writes):
        def fn(e):
            inst = None
            n = len(pairs)
            for i, (l, r) in enumerate(pairs):
                inst = e.matmul(out, l, r, start=(i == 0), stop=(i == n - 1))
            return inst
        S.op("pe", fn, reads=reads, writes=writes)

    def tr(out, in_, idn, reads, writes):
        S.op("pe", lambda e: e.transpose(out, in_, idn), reads=reads, writes=writes)

    def act(out, in_, func, reads, writes, bias=None, scale=None):
        kw = {}
        if bias is not None:
            kw["bias"] = bias
        if scale is not None:
            kw["scale"] = scale
        S.op("act", lambda e: e.activation(out, in_, func, **kw), reads=reads, writes=writes)

    def tt(out, a, b, op, reads, writes, eng="dve"):
        S.op(eng, lambda e: e.tensor_tensor(out, a, b, op), reads=reads, writes=writes)

    def stt(out, in0, scalar, in1, op0, op1, reads, writes):
        S.op("dve", lambda e: e.scalar_tensor_tensor(out, in0, scalar, in1, op0, op1), reads=reads, writes=writes)

    def ts(out, in0, s1, s2, op0, op1, reads, writes, eng="dve"):
        if op1 is None:
            S.op(eng, lambda e: e.tensor_scalar(out, in0, s1, None, op0), reads=reads, writes=writes)
        else:
            S.op(eng, lambda e: e.tensor_scalar(out, in0, s1, s2, op0, op1), reads=reads, writes=writes)

    def cp(out, in_, reads, writes, eng="dve"):
        S.op(eng, lambda e: e.tensor_copy(out, in_), reads=reads, writes=writes)

    def dma(q, out, in_, reads, writes, sem, **kw):
        S.op(q, lambda e: e.dma_start(out=out, in_=in_, **kw), reads=reads, writes=writes, sem=sem, inc=16)

    def rstd_from_ps(pst, n, inv_d, reads_extra=()):
        (pt, pr) = pst
        rs_t, rs_r = RS.next()
        act(rs_t[:, :n], pt[:, :n], AF.Ln, reads=[pr, R_const], writes=[rs_r], bias=epsc[:, 0:1], scale=inv_d)
        act(rs_t[:, :n], rs_t[:, :n], AF.Exp, reads=[rs_r], writes=[rs_r], scale=-0.5)
        return rs_t, rs_r

    d_c = S.dsem("d_const")
    stg = [sb("stg%d" % i, [128, 128], F32) for i in range(3)]
    R_stg = [S.res("stg%d" % i) for i in range(3)]
    iot = sb("iot", [128, 128], F32)
    R_iot = S.res("iot")
    PHASE0 = off[0]
    col_of = {}
    rows = []

    def add_rows(name, ap, n):
        col_of[name] = len(rows)
        for i in range(n):
            rows.append((ap, i))

    add_rows("ada_b0", ada_b_d[0], 48); add_rows("ada_b1", ada_b_d[1], 48)
    add_rows("npre", npre_d, 32); add_rows("npost", npost_d, 32)
    add_rows("cfbin", cfbin_d, 8); add_rows("dwb", dwb_d, 4); add_rows("lng", lng_d, 4); add_rows("lnb", lnb_d, 4)
    add_rows("cbout", cbout_d, 8); add_rows("scw", scw_d, 12); add_rows("dww", dww_d, 124)
    add_rows("qan", qan_d, 3); add_rows("kvan", kvan_d, 2); add_rows("crow", crow_d, 8); add_rows("cctx", cctx_d, 8)
    col_of["qn"] = len(rows); rows.append(("qn", 0))
    col_of["kn"] = len(rows); rows.append(("kn", 0))
    assert len(rows) <= NCOL, len(rows)
    for i in range(3):
        S.op("dve", lambda e, i=i: e.memset(stg[i][:], 0.0), writes=[R_stg[i]])
    r = 0
    while r < len(rows):
        ap, i0 = rows[r]
        if isinstance(ap, str):
            src = qn_d if ap == "qn" else kn_d
            si, p = divmod(r, 128)
            dma("sp", stg[si][p:p + 1, 0:64], src, [], [R_stg[si]], d_c)
            dma("sp", stg[si][p:p + 1, 64:128], src, [], [R_stg[si]], d_c)
            r += 1
            continue
        n = 1
        while (r + n < len(rows) and rows[r + n][0] is ap and rows[r + n][1] == i0 + n and (r + n) // 128 == r // 128):
            n += 1
        si, p = divmod(r, 128)
        dma("sp", stg[si][p:p + n, :], ap[i0:i0 + n, :], [], [R_stg[si]], d_c)
        r += n
    dma("sp", P64_sb[:], P64_d, [], [R_const], d_c)
    dma("sp", P96_sb[:], P96_d, [], [R_const], d_c)
    S.op("pool", lambda e: e.iota(iot[:], [[1, 128]], base=0, channel_multiplier=-1,
                                   allow_small_or_imprecise_dtypes=True), writes=[R_iot])
    ts(ident_f[:], iot[:], 0.0, None, ALU.is_equal, None, [R_iot], [R_const])
    ts(ident_b[:], iot[:], 0.0, None, ALU.is_equal, None, [R_iot], [R_const])
    S.op("dve", lambda e: e.memset(ones_b[:], 1.0), writes=[R_const])
    S.op("dve", lambda e: e.memset(blk64_b[:], 0.0), writes=[R_const])
    S.op("dve", lambda e: e.memset(blk64_b[0:64, 0:64], 1.0), writes=[R_const])
    S.op("dve", lambda e: e.memset(blk64_b[64:128, 64:128], 1.0), writes=[R_const])
    S.op("dve", lambda e: e.memset(epsc[:], EPS), writes=[R_const])
    for si in range(3):
        pt, pr = PS.next()
        tr(pt[:, 0:128], stg[si][:], ident_f[:], [R_stg[si], R_const], [pr])
        cp(colv[:, si * 128:(si + 1) * 128], pt[:, 0:128], [pr], [R_const])

    def cv_(name, i=0):
        c = col_of[name] + i
        return colv[:, c:c + 1]

    act(cs_b[:, :, 0], colv[:, col_of["crow"]:col_of["crow"] + 8], AF.Silu, [R_const], [R_const])
    act(cs_b[:, :, 1], colv[:, col_of["cctx"]:col_of["cctx"] + 8], AF.Silu, [R_const], [R_const])

    d_aw = [S.dsem("d_aw0"), S.dsem("d_aw1")]
    R_modl = [S.res("mod0"), S.res("mod1")]

    def ada_g(l, awb, R_awb, mpt, mpr, stage=None):
        src_l = ada_w_d[l].rearrange("(kc p) n -> p kc n", p=128)
        for ng in range(12):
            b = ng % 2
            if stage is None:
                dma("pool", awb[b][:], src_l[:, :, ng * 512:(ng + 1) * 512], [], [R_awb[b]], d_aw[b])
            else:
                afw, R_afw, d_af = stage
                dma("sp", afw[b][:], src_l[:, :, ng * 512:(ng + 1) * 512], [], [R_afw[b]], d_af[b])
                cp(awb[b][:, 0:4, :], afw[b][:, 0:4, :], [R_afw[b]], [R_awb[b]])
                act(awb[b][:, 4:8, :], afw[b][:, 4:8, :], AF.Copy, [R_afw[b]], [R_awb[b]])
            for j in range(4):
                nchunk = ng * 4 + j
                mm(mpt[:, nchunk * 2:nchunk * 2 + 2],
                   [(awb[b][:, kc, j * 128:(j + 1) * 128], cs_b[:, kc, :]) for kc in range(8)],
                   [R_awb[b], R_const], [mpr])
            yield
        name = "ada_b%d" % l
        for v in range(2):
            tt(mod[:, l, :, v], mpt[:, 0:96].rearrange("p (j v) -> p j v", v=2)[:, :, v],
               colv[:, col_of[name]:col_of[name] + 48], ALU.add, [mpr, R_const], [R_modl[l]])
        for s_ in range(2):
            for v in range(2):
                npre_c = colv[:, col_of["npre"] + (l * 2 + s_) * 8: col_of["npre"] + (l * 2 + s_) * 8 + 8]
                npost_c = colv[:, col_of["npost"] + (l * 2 + s_) * 8: col_of["npost"] + (l * 2 + s_) * 8 + 8]
                S.op("dve", lambda e, l=l, s_=s_, v=v, npre_c=npre_c: e.scalar_tensor_tensor(
                    Acoef[:, l, s_, :, v], mod[:, l, (3 * s_ + 1) * 8:(3 * s_ + 2) * 8, v], 1.0, npre_c, ALU.add, ALU.mult),
                    reads=[R_modl[l], R_const], writes=[R_modl[l]])
                tt(Gcoef[:, l, s_, :, v], mod[:, l, (3 * s_ + 2) * 8:(3 * s_ + 3) * 8, v], npost_c, ALU.mult,
                   [R_modl[l], R_const], [R_modl[l]])

    awb0 = [sb("awb%d" % i, [128, 8, 512], BF16, at=PHASE0 + 8192 + i * 8192) for i in range(2)]
    afw0 = [sb("afw%d" % i, [128, 8, 512], F32, at=PHASE0 + 24576 + i * 16384) for i in range(2)]
    stage0 = (afw0, [S.res("afw0"), S.res("afw1")], [S.dsem("d_af0"), S.dsem("d_af1")])

    def A_(l, s_, c, v):
        return Acoef[:, l, s_, c, v:v + 1]

    def B_(l, s_, c, v):
        return mod[:, l, 3 * s_ * 8 + c, v:v + 1]

    def G_(l, s_, c, v):
        return Gcoef[:, l, s_, c, v:v + 1]

    def run_all(g):
        for _ in g:
            pass

    def advance(g, k):
        if g is None:
            return None
        for _ in range(k):
            try:
                next(g)
            except StopIteration:
                return None
        return g

    run_all(ada_g(0, awb0, [S.res("awb0"), S.res("awb1")], ps[7], PS_RES[7], stage=stage0))

    S.barrier()
    if stop == "setup":
        S.emit()
        return nc

    def prenorm_g(l, s_, blocks, dst, dst_res, from_input=None):
        pending_store = []
        for (gt, lt, n, v) in blocks:
            xb, xb_r, d_xl, d_xs = XBLK.next()
            if from_input is None:
                dma("sp", xb[:, :, :n], x_scr[:, :, gt:gt + n], rx(gt, n), [xb_r], d_xl)
            else:
                xin, R_xin2, d_xin2 = from_input
                for j4 in range(n // 128):
                    tcid = gt // 128 + j4
                    rows = xs_d[tcid * 128:(tcid + 1) * 128, :] if tcid < NS // 128 else xp_d[(tcid - NS // 128) * 128:(tcid - NS // 128 + 1) * 128, :]
                    for half in range(2):
                        dma("sp", xin[:, half * 512:(half + 1) * 512], rows[:, half * 512:(half + 1) * 512], [], [R_xin2[half]], d_xin2[half])
                    if j4 == 0 and pending_store:
                        pending_store.pop()()
                    for half in range(2):
                        pt, pr = PS.next()
                        for j in range(4):
                            c = half * 4 + j
                            tr(pt[:, j * 128:(j + 1) * 128], xin[:, c * 128:(c + 1) * 128], ident_f[:], [R_xin2[half], R_const], [pr])
                        dstx = xb[:, half * 4:(half + 1) * 4, j4 * 128:(j4 + 1) * 128]
                        srcp = pt[:, :].rearrange("p (j t) -> p j t", j=4)
                        if half == 0:
                            cp(dstx, srcp, [pr], [xb_r])
                        else:
                            act(dstx, srcp, AF.Copy, [pr], [xb_r])
                    yield
                pending_store.append(lambda xb=xb, xb_r=xb_r, gt=gt, n=n, d_xs=d_xs:
                                     dma("sp", x_scr[:, :, gt:gt + n], xb[:, :, :n], [xb_r], rx(gt, n), d_xs))
            pst = PSN.next()
            for c in range(8):
                sq_t, sq_r = TMPB.next()
                act(sq_t[:, :n], xb[:, c, :n], AF.Square, [xb_r], [sq_r])
                S.op("pe", lambda e, c=c, sq_t=sq_t, n=n, pst=pst: e.matmul(pst[0][:, :n], ones_b[:], sq_t[:, :n], start=(c == 0), stop=(c == 7)),
                     reads=[sq_r, R_const], writes=[pst[1]])
                yield
            rs_t, rs_r = rstd_from_ps(pst, n, 1.0 / D)
            yield
            for c in range(8):
                t_t, t_r = TMP.next()
                stt(t_t[:, :n], xb[:, c, :n], A_(l, s_, c, v), rs_t[:, :n], ALU.mult, ALU.mult,
                    [xb_r, rs_r, R_modl[l]], [t_r])
                act(dst[:, c, lt:lt + n], t_t[:, :n], AF.Identity, [t_r, R_modl[l]], dst_res(lt, n), bias=B_(l, s_, c, v), scale=1.0)
                yield
        if pending_store:
            pending_store.pop()()

    def postnorm_g(l, s_, blocks, o, o_res, final=None):
        xbs = {}

        def issue_load(i):
            (gt_, lt_, n_, v_) = blocks[i]
            xbi = XBLK.next()
            dma("sp", xbi[0][:, :, :n_], x_scr[:, :, gt_:gt_ + n_], rx(gt_, n_), [xbi[1]], xbi[2])
            xbs[i] = xbi
        for i in range(min(2, len(blocks))):
            issue_load(i)
        yield
        for bi_, (gt, lt, n, v) in enumerate(blocks):
            xb, xb_r, d_xl, d_xs = xbs[bi_]
            pst = PSN.next()
            for c in range(8):
                sq_t, sq_r = TMPB.next()
                act(sq_t[:, :n], o[:, c, lt:lt + n], AF.Square, o_res(lt, n), [sq_r])
                S.op("pe", lambda e, c=c, sq_t=sq_t, n=n, pst=pst: e.matmul(pst[0][:, :n], ones_b[:], sq_t[:, :n], start=(c == 0), stop=(c == 7)),
                     reads=[sq_r, R_const], writes=[pst[1]])
                yield
            rs_t, rs_r = rstd_from_ps(pst, n, 1.0 / D)
            yield
            for c in range(8):
                t_t, t_r = TMP.next()
                stt(t_t[:, :n], o[:, c, lt:lt + n], G_(l, s_, c, v), rs_t[:, :n], ALU.mult, ALU.mult,
                    o_res(lt, n) + [rs_r, R_modl[l]], [t_r])
                tt(xb[:, c, :n], xb[:, c, :n], t_t[:, :n], ALU.add, [xb_r, t_r], [xb_r])
                yield
            if final is None:
                dma("sp", x_scr[:, :, gt:gt + n], xb[:, :, :n], [xb_r], rx(gt, n), d_xs)
                if bi_ + 2 < len(blocks):
                    issue_load(bi_ + 2)
                continue
            yst, R_yst, d_y = final
            for j4 in range(n // 128):
                tcid = gt // 128 + j4
                b = tcid % 2
                for half in range(2):
                    pt, pr = PS.next()
                    for j in range(4):
                        c = half * 4 + j
                        tr(pt[:, j * 128:(j + 1) * 128], xb[:, c, j4 * 128:(j4 + 1) * 128], ident_f[:], [xb_r, R_const], [pr])
                    if half == 0:
                        cp(yst[b][:, 0:512], pt[:, :], [pr], [R_yst[b]])
                    else:
                        act(yst[b][:, 512:1024], pt[:, :], AF.Copy, [pr], [R_yst[b]])
                dst = ys_d[tcid * 128:(tcid + 1) * 128, :] if tcid < NS // 128 else yp_d[(tcid - NS // 128) * 128:(tcid - NS // 128 + 1) * 128, :]
                dma("sp", dst, yst[b][:], [R_yst[b]], [], d_y[b])
                yield
            if bi_ + 2 < len(blocks):
                issue_load(bi_ + 2)

    def prenorm(l, s_, blocks, dst, dst_res):
        run_all(prenorm_g(l, s_, blocks, dst, dst_res))

    def postnorm_res(l, s_, blocks, o, o_res):
        run_all(postnorm_g(l, s_, blocks, o, o_res))

    class Alloc:
        def __init__(self, items):
            self.free = list(items)

        def take(self, k):
            out, self.free = self.free[:k], self.free[k:]
            return out

        def give(self, items):
            self.free.extend(items)

    def acquire(*needs):
        while True:
            if all(len(a.free) >= k for a, k in needs):
                return [a.take(k) for a, k in needs]
            yield

    def interleave(gens):
        gens = list(gens)
        while gens:
            for g_ in list(gens):
                try:
                    next(g_)
                except StopIteration:
                    gens.remove(g_)

    def make_res(nblk, name):
        rr = [S.res("%s%d" % (name, i)) for i in range(nblk)]

        def f(t0, n):
            return rr[t0 // 256:(t0 + n + 255) // 256]
        return f

    d_wgu = [S.dsem("d_wgu%d" % i) for i in range(3)]
    d_wdb = [S.dsem("d_wdb%d" % i) for i in range(2)]
    d_wst = [S.dsem("d_wst%d" % i) for i in range(2)]
    d_wo = S.dsem("d_wo")

    def wout_phase(l, w_d, bias_name, blocks, cat, cat_res, o, o_res, at, w_t=None, w_rl=None):
        if w_t is None:
            w_t = sb("wout_t", [128, 8, D], BF16, at=at)
            w_rl = [S.res("wout")]
        src = w_d.rearrange("(kc p) n -> p kc n", p=128)
        for h2 in range(2):
            dma("pool", w_t[:, :, h2 * 512:(h2 + 1) * 512], src[:, :, h2 * 512:(h2 + 1) * 512], [], w_rl, d_wo)
        post = None
        for blk in blocks:
            (gt, lt, n, v) = blk
            post_next = postnorm_g(l, 0, [blk], o, o_res)
            next(post_next)
            for nch in range(8):
                pt, pr = PS.next()
                mm(pt[:, :n], [(w_t[:, kc, nch * 128:(nch + 1) * 128], cat[:, kc, lt:lt + n]) for kc in range(8)],
                   w_rl + cat_res(lt, n), [pr])
                if bias_name is not None:
                    act(o[:, nch, lt:lt + n], pt[:, :n], AF.Identity, [pr, R_const], o_res(lt, n), bias=cv_(bias_name, nch), scale=1.0)
                else:
                    act(o[:, nch, lt:lt + n], pt[:, :n], AF.Copy, [pr], o_res(lt, n))
                post = advance(post, 3)
            if post is not None:
                run_all(post)
            post = post_next
        run_all(post)

    def ffn_layer(l, groups, base):
        o0 = base
        hfs = [sb("hf%d" % i, [128, 8, 1280], BF16, at=o0 + i * 20480) for i in range(2)]; o0 += 2 * 20480
        actb = sb("actb", [128, 22, 1280], BF16, at=o0); o0 += 56320
        wgu = [sb("wgu%d" % i, [128, 8, 256], BF16, at=o0 + i * 4096) for i in range(3)]; o0 += 3 * 4096
        wdb = [sb("wdb%d" % i, [128, 22, 128], BF16, at=o0 + i * 5632) for i in range(2)]; o0 += 2 * 5632
        ada = None
        final = None
        if l == 0:
            awb1 = [sb("awbx%d" % i, [128, 8, 512], BF16, at=o0 + i * 8192) for i in range(2)]; o0 += 2 * 8192
            ada = ada_g(1, awb1, [S.res("awbx0"), S.res("awbx1")], ps[7], PS_RES[7])
        else:
            yst = [sb("yst%d" % i, [128, D], F32, at=o0 + i * 4096) for i in range(2)]; o0 += 2 * 4096
            final = (yst, [S.res("yst0"), S.res("yst1")], [S.dsem("d_y0"), S.dsem("d_y1")])
        assert o0 <= ARENA1, o0
        hf_ress = [make_res(5, "hfa"), make_res(5, "hfb")]
        act_res = make_res(5, "actb")
        R_wgu = [S.res("wgu%d" % i) for i in range(3)]
        R_wdb = [S.res("wdb%d" % i) for i in range(2)]
        srcg = wg_d[l].rearrange("(kc p) n -> p kc n", p=128)
        srcu = wu_d[l].rearrange("(kc p) n -> p kc n", p=128)
        srcd = wd_d[l].rearrange("(hc p) n -> p hc n", p=128)
        wi = [0, 0]
        run_all(prenorm_g(l, 1, groups[0], hfs[0], hf_ress[0]))
        post = None
        for gi, blocks in enumerate(groups):
            hf = hfs[gi % 2]; hf_res = hf_ress[gi % 2]
            for hc in range(22):
                b = wi[0] % 3; wi[0] += 1
                dma("pool", wgu[b][:, :, 0:128], srcg[:, :, hc * 128:(hc + 1) * 128], [], [R_wgu[b]], d_wgu[b])
                dma("pool", wgu[b][:, :, 128:256], srcu[:, :, hc * 128:(hc + 1) * 128], [], [R_wgu[b]], d_wgu[b])
                for (gt, lt, n, v) in blocks:
                    pg, pgr = PS.next()
                    pu, pur = PS.next()
                    mm(pg[:, :n], [(wgu[b][:, kc, 0:128], hf[:, kc, lt:lt + n]) for kc in range(8)], [R_wgu[b]] + hf_res(lt, n), [pgr])
                    mm(pu[:, :n], [(wgu[b][:, kc, 128:256], hf[:, kc, lt:lt + n]) for kc in range(8)], [R_wgu[b]] + hf_res(lt, n), [pur])
                    t_t, t_r = TMP.next()
                    act(t_t[:, :n], pg[:, :n], AF.Silu, [pgr], [t_r])
                    tt(actb[:, hc, lt:lt + n], pu[:, :n], t_t[:, :n], ALU.mult, [pur, t_r], act_res(lt, n))
                    post = advance(post, 1)
            run_all(post) if post is not None else None
            post = None
            pre = None
            post_last = None
            if gi + 1 < len(groups):
                pre = prenorm_g(l, 1, groups[gi + 1], hfs[(gi + 1) % 2], hf_ress[(gi + 1) % 2])
            else:
                post_last = postnorm_g(l, 1, blocks, hf, hf_res, final=final)
                next(post_last)
            for nch in range(8):
                b = wi[1] % 2; wi[1] += 1
                dma("pool", wdb[b][:], srcd[:, :, nch * 128:(nch + 1) * 128], [], [R_wdb[b]], d_wdb[b])
                for (gt, lt, n, v) in blocks:
                    pt, pr = PS.next()
                    mm(pt[:, :n], [(wdb[b][:, hc, :], actb[:, hc, lt:lt + n]) for hc in range(22)], [R_wdb[b]] + act_res(lt, n), [pr])
                    act(hf[:, nch, lt:lt + n], pt[:, :n], AF.Copy, [pr], hf_res(lt, n))
                    pre = advance(pre, 3)
                ada = advance(ada, 1)
            run_all(pre) if pre is not None else None
            post = post_last if post_last is not None else postnorm_g(l, 1, blocks, hf, hf_res, final=final)
        run_all(post)
        if ada is not None:
            run_all(ada)

    GA = [(0, 0, 512, 0), (512, 512, 512, 0)]
    GB = [(1024, 0, 512, 0), (1536, 512, 512, 0)]
    GC = [(2048, 0, 256, 1), (2304, 256, 256, 1)]

    def conv_layer(blocks, ntok, pads31, pads3):
        o0 = PHASE0
        h = sb("h0", [128, 8, NT], BF16, at=o0); o0 += 8 * NT * 2
        cat = sb("cat0", [128, 8, NT], BF16, at=o0); o0 += 8 * NT * 2
        zpad = sb("zpad", [128, 2624], BF16, at=o0); o0 += 5248
        ppad = sb("ppad", [128, 2624], BF16, at=o0); o0 += 5248
        gb = sb("gb", [128, NT], F32, at=o0); o0 += NT * 4
        dg = sb("dg", [128, 34, 128], BF16, at=o0); o0 += 34 * 256
        wst = [sb("wst%d" % i, [128, 8, 384], BF16, at=o0 + i * 6144) for i in range(2)]; o0 += 2 * 6144
        wout_at = o0; o0 += 16384
        xin_c = sb("xin_c", [128, D], F32, at=o0); o0 += 4096
        assert o0 <= ARENA1, o0
        h_res = make_res(10, "h0"); cat_res = make_res(10, "cat0"); catB_res = make_res(10, "cat0B")
        R_zpad = S.res("zpad"); R_ppad = S.res("ppad"); R_gb = S.res("gb"); R_dg = S.res("dg")
        R_wst = [S.res("wst0"), S.res("wst1")]
        S.op("dve", lambda e: e.memset(zpad[:], 0.0), writes=[R_zpad])
        S.op("dve", lambda e: e.memset(ppad[:], 0.0), writes=[R_ppad])
        pre = prenorm_g(0, 0, blocks, h, h_res, from_input=(xin_c, [S.res("xin_c0"), S.res("xin_c1")], [S.dsem("d_xin0"), S.dsem("d_xin1")]))
        nst = [n // 128 + 17 for (_, _, n, _) in blocks]
        srcw = cwin_d.rearrange("(kc p) n -> p kc n", p=128)
        wi = 0

        def b_in(ci, b, bi):
            (gt, lt, n, v) = blocks[bi]
            pa, par = PS.next()
            pg, pgr = PS.next()
            mm(pa[:, :n], [(wst[b][:, kc, 0:128], h[:, kc, lt:lt + n]) for kc in range(8)], [R_wst[b]] + h_res(lt, n), [par])
            mm(pg[:, :n], [(wst[b][:, kc, 128:256], h[:, kc, lt:lt + n]) for kc in range(8)], [R_wst[b]] + h_res(lt, n), [pgr])
            t_t, t_r = TMP.next()
            act(t_t[:, :n], pg[:, :n], AF.Sigmoid, [pgr, R_const], [t_r], bias=cv_("cfbin", 4 + ci), scale=1.0)
            stt(zpad[:, pads31[bi]:pads31[bi] + n], pa[:, :n], cv_("cfbin", ci), t_t[:, :n], ALU.add, ALU.mult,
                [par, t_r, R_const], [R_zpad])

        def b_dw(ci, bi):
            (gt, lt, n, v) = blocks[bi]
            pt, pr = PS.next()
            p0 = pads31[bi] - 15
            mm(pt[:, :n], [(dg[:, j, :], zpad[:, p0 + j:p0 + j + n]) for j in range(31)], [R_dg, R_zpad], [pr])
            act(cat[:, 4 + ci, lt:lt + n], pt[:, :n], AF.Identity, [pr, R_const], catB_res(lt, n), bias=cv_("dwb", ci), scale=1.0)

        nb = len(blocks)
        for ci in range(4):
            b = wi % 2; wi += 1
            dma("pool", wst[b][:, :, 0:128], srcw[:, :, 1536 + ci * 128:1536 + (ci + 1) * 128], [], [R_wst[b]], d_wst[b])
            dma("pool", wst[b][:, :, 128:256], srcw[:, :, 2048 + ci * 128:2048 + (ci + 1) * 128], [], [R_wst[b]], d_wst[b])
            for j in range(31):
                ts(dg[:, j, :], ident_b[:], cv_("dww", j * 4 + ci), None, ALU.mult, None, [R_const], [R_dg])
            if ci == 0:
                pre = advance(pre, nst[0])
                for bi in range(nb):
                    if bi + 1 < nb:
                        pre = advance(pre, nst[bi + 1] - 17)
                        b_in(ci, b, bi)
                        pre = advance(pre, 9)
                        if bi >= 1:
                            b_dw(ci, bi - 1)
                        pre = advance(pre, 8)
                    else:
                        b_in(ci, b, bi)
                        b_dw(ci, bi - 1)
                        b_dw(ci, bi)
                if pre is not None:
                    run_all(pre)
                continue
            for bi in range(nb):
                b_in(ci, b, bi)
            for bi in range(nb):
                b_dw(ci, bi)
        def ln_g():
            (m_t, m_r), (v_t, v_r), (rs_t, rs_r) = RS.items[0], RS.items[1], RS.items[2]
            p1 = (ps[6], PS_RES[6]); p2 = (ps[7], PS_RES[7])
            for (gt, lt, n, v) in blocks:
                sq_t, sq_r = SQB.next()
                act(sq_t[:, 0:4, :n], cat[:, 4:8, lt:lt + n], AF.Square, catB_res(lt, n), [sq_r]); yield
                mm(p1[0][:, :n], [(ones_b[:], cat[:, 4 + ci, lt:lt + n]) for ci in range(4)], catB_res(lt, n) + [R_const], [p1[1]]); yield
                mm(p2[0][:, :n], [(ones_b[:], sq_t[:, ci, :n]) for ci in range(4)], [sq_r, R_const], [p2[1]]); yield
                act(m_t[:, :n], p1[0][:, :n], AF.Copy, [p1[1]], [m_r], scale=1.0 / 512); yield
                tt(v_t[:, :n], m_t[:, :n], m_t[:, :n], ALU.mult, [m_r], [v_r]); yield
                stt(v_t[:, :n], p2[0][:, :n], 1.0 / 512, v_t[:, :n], ALU.mult, ALU.subtract, [p2[1], v_r], [v_r]); yield
                act(rs_t[:, :n], v_t[:, :n], AF.Ln, [v_r, R_const], [rs_r], bias=epsc[:, 0:1], scale=1.0); yield
                act(rs_t[:, :n], rs_t[:, :n], AF.Exp, [rs_r], [rs_r], scale=-0.5); yield
                for ci in range(4):
                    tt(v_t[:, :n], cat[:, 4 + ci, lt:lt + n], m_t[:, :n], ALU.subtract, catB_res(lt, n) + [m_r], [v_r]); yield
                    tt(v_t[:, :n], v_t[:, :n], rs_t[:, :n], ALU.mult, [v_r, rs_r], [v_r]); yield
                    act(cat[:, 4 + ci, lt:lt + n], v_t[:, :n], AF.Silu, [v_r, R_const], catB_res(lt, n),
                        bias=cv_("lnb", ci), scale=cv_("lng", ci)); yield

        ln = ln_g()

        for ci in range(4):
            b = wi % 2; wi += 1
            for j3 in range(3):
                dma("pool", wst[b][:, :, j3 * 128:(j3 + 1) * 128], srcw[:, :, j3 * 512 + ci * 128:j3 * 512 + (ci + 1) * 128],
                    [], [R_wst[b]], d_wst[b])
            for j in range(3):
                ts(dg[:, 31 + j, :], ident_b[:], cv_("scw", j * 4 + ci), None, ALU.mult, None, [R_const], [R_dg])
            for bi, (gt, lt, n, v) in enumerate(blocks):
                pb, pbr = PS.next(); pc, pcr = PS.next(); px, pxr = PS.next()
                mm(pb[:, :n], [(wst[b][:, kc, 0:128], h[:, kc, lt:lt + n]) for kc in range(8)], [R_wst[b]] + h_res(lt, n), [pbr])
                mm(pc[:, :n], [(wst[b][:, kc, 128:256], h[:, kc, lt:lt + n]) for kc in range(8)], [R_wst[b]] + h_res(lt, n), [pcr])
                mm(px[:, :n], [(wst[b][:, kc, 256:384], h[:, kc, lt:lt + n]) for kc in range(8)], [R_wst[b]] + h_res(lt, n), [pxr])
                t_t, t_r = TMP.next()
                act(t_t[:, :n], px[:, :n], AF.Copy, [pxr], [t_r])
                tt(ppad[:, pads3[bi]:pads3[bi] + n], pc[:, :n], t_t[:, :n], ALU.mult, [pcr, t_r], [R_ppad])
                act(gb[:, lt:lt + n], pb[:, :n], AF.Copy, [pbr], [R_gb])
                ln = advance(ln, 3)
            for bi, (gt, lt, n, v) in enumerate(blocks):
                pt, pr = PS.next()
                p0 = pads3[bi] - 1
                mm(pt[:, :n], [(dg[:, 31 + j, :], ppad[:, p0 + j:p0 + j + n]) for j in range(3)], [R_dg, R_ppad], [pr])
                tt(cat[:, ci, lt:lt + n], pt[:, :n], gb[:, lt:lt + n], ALU.mult, [pr, R_gb], cat_res(lt, n))
                ln = advance(ln, 3)
        if ln is not None:
            run_all(ln)
        wout_phase(0, cwout_d, "cbout", blocks, cat, lambda lt, n: cat_res(lt, n) + catB_res(lt, n), h, h_res, wout_at)

    blocksAll = [(i * 512, i * 512, 512, 0) for i in range(4)] + [(2048, 2048, 256, 1), (2304, 2304, 256, 1)]
    conv_layer(blocksAll, NT, [15 + 512 * i for i in range(4)] + [2078, 2349], [1 + 512 * i for i in range(4)] + [2050, 2307])
    S.barrier()
    if stop == "conv":
        S.emit()
        return nc
    GCm = [(2048, 0, 512, 1)]
    GAf = [(0, 0, 512, 0), (512, 512, 512, 0), (2048, 1024, 256, 1)]
    GBf = [(1024, 0, 512, 0), (1536, 512, 512, 0), (2304, 1024, 256, 1)]
    ffn_layer(0, [GAf, GBf], PHASE0)
    S.barrier()
    if stop in ("layer0", "skipconvS"):
        S.emit()
        return nc

    o0 = PHASE0
    KG = sb("KG", [128, 2, NK], BF16, at=o0); o0 += 2 * NK * 2
    VG = sb("VG", [128, 22, 2, 128], BF16, at=o0); o0 += 22 * 2 * 128 * 2
    ckvT = sb("ckvT", [128, 2, NK], BF16, at=o0); o0 += 2 * NK * 2
    KM = [sb("KM0", [96, NK], BF16, at=o0)] * 2; o0 += NK * 2
    ATT0 = o0
    R_KG = S.res("KG"); R_VG = S.res("VG"); R_ckvT = S.res("ckvT"); R_KM = [S.res("KM0")] * 2

    def kidx(gt):
        return 256 + gt

    RHl = [S.res("hscr%d" % i) for i in range(NT // 256)]

    def RH(t0, n):
        return RHl[t0 // 256:(t0 + n + 255) // 256]
    d_hs = S.dsem("d_hs")

    def kside():
        o1 = ATT0
        hg = sb("hgk", [128, 8, 1024], BF16, at=o1); o1 += 16384
        wk = sb("wk", [128, 8, 2, 128], BF16, at=o1); o1 += 8 * 256 * 2
        wkraw = sb("wkraw", [128, 8, 128], BF16, at=o1); o1 += 8 * 128 * 2
        wv = sb("wv", [128, 8, 128], BF16, at=o1); o1 += 8 * 128 * 2
        wckv = sb("wckv", [128, 8, 256], BF16, at=o1); o1 += 8 * 256 * 2
        wkrraw = sb("wkrraw", [128, 8, 128], BF16, at=o1); o1 += 8 * 128 * 2
        cstg = sb("cstg", [128, 2, 768], F32, at=o1); o1 += 2 * 768 * 4
        ostg = sb("ostg", [128, 4, 544], F32, at=o1); o1 += 4 * 544 * 4
        assert o1 <= ARENA1, o1
        hg_res = make_res(4, "hgk")
        R_w = S.res("wkside"); R_cstg = S.res("cstg"); R_ostg = S.res("ostg")
        d_k = S.dsem("d_k"); d_k2 = S.dsem("d_k2"); d_rope = S.dsem("d_ropek"); d_o = S.dsem("d_ko")
        srcw = awin_d.rearrange("(kc p) n -> p kc n", p=128)
        R_wraw = S.res("wkraw")
        dma("pool", wkraw[:], srcw[:, :, 512:640], [], [R_wraw], d_k)
        dma("pool", wv[:], srcw[:, :, 640:768], [], [R_w], d_k)
        dma("pool", wckv[:], srcw[:, :, 1152:1408], [], [R_w], d_k)
        dma("pool", wkrraw[:], srcw[:, :, 1312:1440], [], [R_w], d_k)
        wkr = wkrraw[:, :, 32:128]
        for g in range(2):
            for dup in range(2):
                cp(wk[:, :, g, dup * 64:(dup + 1) * 64], wkraw[:, :, g * 64:(g + 1) * 64], [R_wraw], [R_w])
        S.op("dve", lambda e: e.memset(VG[:], 1.0), writes=[R_VG])
        S.op("dve", lambda e: e.memset(cstg[:], 0.0), writes=[R_cstg])
        for ch in range(2):
            dma("sp", cstg[:, ch, 640:768], cv_d[ch * 128:(ch + 1) * 128, :], [], [R_cstg], d_k2)
        for ch in range(2):
            for g in range(2):
                for dup in range(2):
                    dma("sp", cstg[:, ch, g * 128 + dup * 64:g * 128 + (dup + 1) * 64], ck_d[ch * 128:(ch + 1) * 128, g * 64:(g + 1) * 64],
                        [], [R_cstg], d_k2)
            dma("sp", cstg[:, ch, 256:512], cckv_d[ch * 128:(ch + 1) * 128, :], [], [R_cstg], d_k2)
            dma("sp", cstg[:, ch, 512 + 64:512 + 96], ckr_d[ch * 128:(ch + 1) * 128, :], [], [R_cstg], d_k2)
        for ch in range(2 if "k_cache" not in DEBUG_SKIP else 0):
            cp(VG[:, ch, :, 0:64], cstg[:, ch, 640:768].rearrange("p (g d) -> p g d", g=2), [R_cstg], [R_VG])
            for g in range(2):
                pt, pr = PS.next()
                tr(pt[:, 0:128], cstg[:, ch, g * 128:(g + 1) * 128], ident_f[:], [R_cstg, R_const], [pr])
                cp(KG[:, g, ch * 128:(ch + 1) * 128], pt[:, 0:128], [pr], [R_KG])
            for cc in range(2):
                pt, pr = PS.next()
                tr(pt[:, 0:128], cstg[:, ch, 256 + cc * 128:256 + (cc + 1) * 128], ident_f[:], [R_cstg, R_const], [pr])
                cp(ckvT[:, cc, ch * 128:(ch + 1) * 128], pt[:, 0:128], [pr], [R_ckvT])
            pt, pr = PS.next()
            tr(pt[0:96, 0:128], cstg[:, ch, 512:608], ident_f[:], [R_cstg, R_const], [pr])
            cp(KM[0][64:96, ch * 128:(ch + 1) * 128], pt[64:96, 0:128], [pr], [R_KM[0]])
        BK = Alloc([(ps[i], PS_RES[i]) for i in range(8) if i != 6])
        KT = Alloc([(sb("kt%d" % i, [128, 512], F32, at=o1 + i * 2048), S.res("kt%d" % i)) for i in range(10)])
        o1 += 10 * 2048
        KRS = Alloc([(sb("krs%d" % i, [128, 512], F32, at=o1 + i * 2048), S.res("krs%d" % i)) for i in range(4)])
        o1 += 4 * 2048
        KSQ = Alloc([(sb("ksq%d" % i, [128, 512], BF16, at=o1 + i * 1024), S.res("ksq%d" % i)) for i in range(4)])
        o1 += 4 * 1024
        ropeT = []
        for i in range(2):
            ropeT.append((sb("ropeCk%d" % i, [128, 512], F32, at=o1), sb("ropeSk%d" % i, [128, 512], F32, at=o1 + 2048),
                          sb("rC96k%d" % i, [96, 512], F32, at=o1 + 4096), sb("rS96k%d" % i, [96, 512], F32, at=o1 + 6144),
                          S.res("ropek%d" % i)))
            o1 += 8192
        assert o1 <= ARENA1, o1

        def rstd_chain(pst, rs, n, inv_d):
            act(rs[0][:, :n], pst[0][:, :n], AF.Ln, [pst[1], R_const], [rs[1]], bias=epsc[:, 0:1], scale=inv_d)
            yield
            act(rs[0][:, :n], rs[0][:, :n], AF.Exp, [rs[1]], [rs[1]], scale=-0.5)
            yield

        def chain_k(g, blk, tabs):
            (gt, lt, n, v) = blk
            k0 = kidx(gt); is_s = (v == 0)
            (ropeC, ropeS, rC96, rS96, R_rope) = tabs
            got = yield from acquire((BK, 2), (KT, 3), (KRS, 1), (KSQ, 1))
            (pk, pkr), (p2, p2r) = got[0]
            (kn_t, kn_r), (a_t, a_r), (b_t, b_r) = got[1]
            rs = got[2][0]
            sq_t, sq_r = got[3][0]
            mm(pk[:, :n], [(wk[:, kc, g, :], hg[:, kc, lt:lt + n]) for kc in range(8)], [R_w] + hg_res(lt, n), [pkr]); yield
            act(sq_t[:, :n], pk[:, :n], AF.Square, [pkr], [sq_r]); yield
            mm(p2[:, :n], [(blk64_b[:], sq_t[:, :n])], [sq_r, R_const], [p2r]); yield
            yield from rstd_chain((p2, p2r), rs, n, 1.0 / 64)
            stt(kn_t[:, :n], pk[:, :n], cv_("kn"), rs[0][:, :n], ALU.mult, ALU.mult, [pkr, rs[1], R_const], [kn_r]); yield
            if is_s:
                mm(p2[:, :n], [(P64_sb[:], kn_t[:, :n])], [kn_r, R_const], [p2r]); yield
                tt(a_t[:, :n], kn_t[:, :n], ropeC[:, :n], ALU.mult, [kn_r, R_rope], [a_r]); yield
                tt(b_t[:, :n], p2[:, :n], ropeS[:, :n], ALU.mult, [p2r, R_rope], [b_r]); yield
                tt(KG[:, g, k0:k0 + n], a_t[:, :n], b_t[:, :n], ALU.add, [a_r, b_r], [R_KG]); yield
            else:
                cp(KG[:, g, k0:k0 + n], kn_t[:, :n], [kn_r], [R_KG]); yield
                for t4 in range(n // 128):
                    tr(p2[:, 0:128], kn_t[:, t4 * 128:(t4 + 1) * 128], ident_f[:], [kn_r, R_const], [p2r]); yield
                    slot = ((gt - NS) // 128 + t4)
                    act(ostg[:, slot, g * 64:(g + 1) * 64], p2[:, 0:64], AF.Copy, [p2r], [R_ostg]); yield
            BK.give(got[0]); KT.give(got[1]); KRS.give(got[2]); KSQ.give(got[3])

        def chain_v(blk):
            (gt, lt, n, v) = blk
            k0 = kidx(gt); is_s = (v == 0)
            got = yield from acquire((BK, 1))
            (pt, pr) = got[0][0]
            for t4 in range(n // 128):
                mm(pt[:, 0:128], [(hg[:, kc, lt + t4 * 128:lt + (t4 + 1) * 128], wv[:, kc, :]) for kc in range(8)],
                   [R_w] + hg_res(lt, n), [pr]); yield
                kc_ = (k0 + t4 * 128) // 128
                cp(VG[:, kc_, :, 0:64], pt[:, 0:128].rearrange("p (g d) -> p g d", g=2), [pr], [R_VG]); yield
                if not is_s:
                    slot = ((gt - NS) // 128 + t4)
                    cp(ostg[:, slot, 128:256], pt[:, 0:128], [pr], [R_ostg]); yield
            BK.give(got[0])

        def chain_ckv(blk):
            (gt, lt, n, v) = blk
            k0 = kidx(gt); is_s = (v == 0)
            got = yield from acquire((BK, 3), (KT, 2), (KRS, 1))
            pcs = got[0][0:2]; (p3, p3r) = got[0][2]
            cts = got[1]; rs = got[2][0]
            sq_t, sq_r = SQB.next()
            for cc in range(2):
                mm(pcs[cc][0][:, :n], [(wckv[:, kc, cc * 128:(cc + 1) * 128], hg[:, kc, lt:lt + n]) for kc in range(8)],
                   [R_w] + hg_res(lt, n), [pcs[cc][1]]); yield
                act(sq_t[:, cc, :n], pcs[cc][0][:, :n], AF.Square, [pcs[cc][1]], [sq_r]); yield
            mm(p3[:, :n], [(ones_b[:], sq_t[:, cc, :n]) for cc in range(2)], [sq_r, R_const], [p3r]); yield
            yield from rstd_chain((p3, p3r), rs, n, 1.0 / 256)
            for cc in range(2):
                c_t, c_r = cts[cc]
                stt(c_t[:, :n], pcs[cc][0][:, :n], cv_("kvan", cc), rs[0][:, :n], ALU.mult, ALU.mult,
                    [pcs[cc][1], rs[1], R_const], [c_r]); yield
                act(ckvT[:, cc, k0:k0 + n], c_t[:, :n], AF.Copy, [c_r], [R_ckvT]); yield
                if not is_s:
                    for t4 in range(n // 128):
                        tr(p3[:, 0:128], c_t[:, t4 * 128:(t4 + 1) * 128], ident_f[:], [c_r, R_const], [p3r]); yield
                        slot = ((gt - NS) // 128 + t4)
                        cp(ostg[:, slot, 256 + cc * 128:256 + (cc + 1) * 128], p3[:, 0:128], [p3r], [R_ostg]); yield
            BK.give(got[0]); KT.give(got[1]); KRS.give(got[2])

        def chain_kr(blk, tabs):
            (gt, lt, n, v) = blk
            k0 = kidx(gt); is_s = (v == 0)
            (ropeC, ropeS, rC96, rS96, R_rope) = tabs
            got = yield from acquire((BK, 2), (KT, 3))
            (pk, pkr), (p2, p2r) = got[0]
            (kr_t, kr_r), (a_t, a_r), (b_t, b_r) = got[1]
            mm(pk[0:96, :n], [(wkrraw[:, kc, 32:128], hg[:, kc, lt:lt + n]) for kc in range(8)], [R_w] + hg_res(lt, n), [pkr]); yield
            act(kr_t[0:96, :n], pk[0:96, :n], AF.Copy, [pkr], [kr_r]); yield
            if is_s:
                mm(p2[0:96, :n], [(P96_sb[:], kr_t[0:96, :n])], [kr_r, R_const], [p2r]); yield
                tt(a_t[64:96, :n], kr_t[64:96, :n], rC96[64:96, :n], ALU.mult, [kr_r, R_rope], [a_r]); yield
                tt(b_t[64:96, :n], p2[64:96, :n], rS96[64:96, :n], ALU.mult, [p2r, R_rope], [b_r]); yield
                tt(KM[0][64:96, k0:k0 + n], a_t[64:96, :n], b_t[64:96, :n], ALU.add, [a_r, b_r], [R_KM[0]]); yield
            else:
                cp(KM[0][64:96, k0:k0 + n], kr_t[64:96, :n], [kr_r], [R_KM[0]]); yield
                for t4 in range(n // 128):
                    tr(p2[:, 0:96], kr_t[0:96, t4 * 128:(t4 + 1) * 128], ident_f[0:96, 0:96], [kr_r, R_const], [p2r]); yield
                    slot = ((gt - NS) // 128 + t4)
                    act(ostg[:, slot, 512:544], p2[:, 64:96], AF.Copy, [p2r], [R_ostg]); yield
            BK.give(got[0]); KT.give(got[1])

        kblocks = []
        for bi, (gt, _lt, n, v) in enumerate(GA + GB + GC):
            kblocks.append((gt, (bi % 2) * 512, n, v))
        run_all(prenorm_g(1, 0, [kblocks[0]], hg, hg_res))
        for bi, blk in enumerate(kblocks):
            (gt, lt, n, v) = blk
            dma("sp", h_scr[:, :, gt:gt + n], hg[:, :, lt:lt + n], hg_res(lt, n), RH(gt, n), d_hs)
            tabs = ropeT[bi % 2]
            if v == 0:
                dma("sp", tabs[0][:, :n], C64_d[:, gt:gt + n], [], [tabs[4]], d_rope)
                dma("sp", tabs[1][:, :n], S64_d[:, gt:gt + n], [], [tabs[4]], d_rope)
                dma("sp", tabs[2][:, :n], C96_d[:, gt:gt + n], [], [tabs[4]], d_rope)
                dma("sp", tabs[3][:, :n], S96_d[:, gt:gt + n], [], [tabs[4]], d_rope)
            chains = [chain_k(0, blk, tabs), chain_k(1, blk, tabs), chain_v(blk), chain_ckv(blk), chain_kr(blk, tabs)]
            if bi + 1 < len(kblocks):
                chains.append(prenorm_g(1, 0, [kblocks[bi + 1]], hg, hg_res))
            interleave(chains)

        for slot in range(4 if "k_out" not in DEBUG_SKIP else 0):
            dma("sp", nk_d[slot * 128:(slot + 1) * 128, :], ostg[:, slot, 0:128], [R_ostg], [], d_o)
            dma("sp", nv_d[slot * 128:(slot + 1) * 128, :], ostg[:, slot, 128:256], [R_ostg], [], d_o)
            dma("sp", nckv_d[slot * 128:(slot + 1) * 128, :], ostg[:, slot, 256:512], [R_ostg], [], d_o)
            dma("sp", nkr_d[slot * 128:(slot + 1) * 128, :], ostg[:, slot, 512:544], [R_ostg], [], d_o)

    kside()
    S.barrier()
    if stop == "kside":
        S.emit()
        return nc

    def attn_all():
        o1 = ATT0
        hg = sb("hgq", [128, 8, 1024], BF16, at=o1); o1 += 16384
        cat = sb("cat1", [128, 8, 1024], BF16, at=o1); o1 += 16384
        QG = sb("QG", [128, 8, 1024], BF16, at=o1); o1 += 16384
        qa = sb("qa", [128, 3, 1024], BF16, at=o1); o1 += 6144
        QM = [sb("QM%d" % i, [96, 1024], BF16, at=o1 + i * 2048) for i in range(2)]; o1 += 4096
        VM0_ = sb("VM0", [128, 22, 128], BF16, at=o1); o1 += 5632
        KM1_ = sb("KM1", [96, NK], BF16, at=ATT0)
        VM1_ = sb("VM1", [128, 22, 128], BF16, at=ATT0 + 5632)
        VM = [VM0_, VM1_]
        KMq = [KM[0], KM1_]
        wqb = sb("wqb", [128, 3, 768], BF16, at=o1); o1 += 4608
        wkvb = sb("wkvb", [128, 2, 1024], BF16, at=o1); o1 += 4096
        rC96 = sb("rC96q", [96, 1024], F32, at=o1); o1 += 4096
        rS96 = sb("rS96q", [96, 1024], F32, at=o1); o1 += 4096
        pT = [sb("pT%d" % i, [128, 512], BF16, at=o1 + i * 1024) for i in range(4)]; o1 += 4096
        wout_at = o1
        wq = sb("wq", [128, 8, 512], BF16, at=o1); o1 += 8192
        wqa = sb("wqa", [128, 8, 384], BF16, at=o1); o1 += 6144
        ropeC = sb("ropeCq", [128, 512], F32, at=o1); o1 += 2048
        ropeS = sb("ropeSq", [128, 512], F32, at=o1); o1 += 2048
        assert o1 <= ARENA1, o1
        PT = Rot([(pT[i], S.res("pT%d" % i)) for i in range(4)])
        hg_res = make_res(4, "hgq"); cat_res = make_res(4, "cat1")
        R_QG = make_res(4, "QG"); R_qa = make_res(4, "qa")
        R_QM = [S.res("QM0"), S.res("QM1")]
        R_VMl = [[S.res("VM0")], [S.res("VM1")] + hg_res(0, 1024)]
        R_KMl = [[R_KM[0]], [S.res("KM1")] + hg_res(0, 1024)]
        R_w = S.res("wqside"); R_rope = S.res("ropeq"); R_r96 = S.res("rope96q")
        d_q = S.dsem("d_q"); d_rope = S.dsem("d_ropeq")
        srcw = awin_d.rearrange("(kc p) n -> p kc n", p=128)
        dma("pool", wq[:], srcw[:, :, 0:512], [], [R_w], d_q)
        dma("pool", wqa[:], srcw[:, :, 768:1152], [], [R_w], d_q)
        dma("pool", wqb[:], wqb_d.rearrange("(kc p) n -> p kc n", p=128), [], [R_w], d_q)
        dma("pool", wkvb[:, :, 0:512], wkvb_d.rearrange("(kc p) n -> p kc n", p=128)[:, :, 0:512], [], [R_w], d_q)
        dma("pool", wkvb[:, :, 512:1024], wkvb_d.rearrange("(kc p) n -> p kc n", p=128)[:, :, 512:1024], [], [R_w], d_q)
        BKq = Alloc([(ps[i], PS_RES[i]) for i in range(8)])
        KTq = Alloc(list(TMP.items)[:4])
        KRSq = Alloc(list(RS.items))
        KSQq = Alloc(list(TMPB.items))

        def group(G):
            ng = max(lt + n for (_, lt, n, _) in G)
            is_s = (G[0][3] == 0)
            if is_s:
                g0 = G[0][0]
                dma("sp", rC96[:, :ng], C96_d[:, g0:g0 + ng], [], [R_r96], d_rope)
                dma("sp", rS96[:, :ng], S96_d[:, g0:g0 + ng], [], [R_r96], d_rope)
            S.op("dve", lambda e: e.memset(VM[0][:], 1.0), writes=R_VMl[0])
            S.op("dve", lambda e: e.memset(QG[:], 0.0), writes=R_QG(0, 1024))
            for (gt, lt, n, v) in G:
                dma("sp", hg[:, :, lt:lt + n], h_scr[:, :, gt:gt + n], RH(gt, n), hg_res(lt, n), d_rope)
            def chain_q(qc, blk):
                (gt, lt, n, v) = blk
                got = yield from acquire((BKq, 2), (KTq, 2), (KRSq, 1), (KSQq, 1))
                (pq, pqr), (p2, p2r) = got[0]
                (qn_t, qn_r), (b_t, b_r) = got[1]
                rs = got[2][0]
                sq_t, sq_r = got[3][0]
                mm(pq[:, :n], [(wq[:, kc, qc * 128:(qc + 1) * 128], hg[:, kc, lt:lt + n]) for kc in range(8)], [R_w] + hg_res(lt, n), [pqr]); yield
                act(sq_t[:, :n], pq[:, :n], AF.Square, [pqr], [sq_r]); yield
                mm(p2[:, :n], [(blk64_b[:], sq_t[:, :n])], [sq_r, R_const], [p2r]); yield
                act(rs[0][:, :n], p2[:, :n], AF.Ln, [p2r, R_const], [rs[1]], bias=epsc[:, 0:1], scale=1.0 / 64); yield
                act(rs[0][:, :n], rs[0][:, :n], AF.Exp, [rs[1]], [rs[1]], scale=-0.5); yield
                stt(qn_t[:, :n], pq[:, :n], cv_("qn"), rs[0][:, :n], ALU.mult, ALU.mult, [pqr, rs[1], R_const], [qn_r]); yield
                if v == 0:
                    mm(p2[:, :n], [(P64_sb[:], qn_t[:, :n])], [qn_r, R_const], [p2r]); yield
                    tt(b_t[:, :n], p2[:, :n], ropeS[:, :n], ALU.mult, [p2r, R_rope], [b_r]); yield
                    tt(qn_t[:, :n], qn_t[:, :n], ropeC[:, :n], ALU.mult, [qn_r, R_rope], [qn_r]); yield
                    tt(QG[0:64, 2 * qc, lt:lt + n], qn_t[0:64, :n], b_t[0:64, :n], ALU.add, [qn_r, b_r], R_QG(lt, n)); yield
                    tt(QG[64:128, 2 * qc + 1, lt:lt + n], qn_t[64:128, :n], b_t[64:128, :n], ALU.add, [qn_r, b_r], R_QG(lt, n)); yield
                else:
                    cp(QG[0:64, 2 * qc, lt:lt + n], qn_t[0:64, :n], [qn_r], R_QG(lt, n)); yield
                    cp(QG[64:128, 2 * qc + 1, lt:lt + n], qn_t[64:128, :n], [qn_r], R_QG(lt, n)); yield
                BKq.give(got[0]); KTq.give(got[1]); KRSq.give(got[2]); KSQq.give(got[3])

            def chain_qa(blk):
                (gt, lt, n, v) = blk
                got = yield from acquire((BKq, 4), (KRSq, 1))
                pcs = got[0][0:3]; (p3, p3r) = got[0][3]
                rs = got[1][0]
                sq_t, sq_r = SQB.next()
                for cc in range(3):
                    mm(pcs[cc][0][:, :n], [(wqa[:, kc, cc * 128:(cc + 1) * 128], hg[:, kc, lt:lt + n]) for kc in range(8)],
                       [R_w] + hg_res(lt, n), [pcs[cc][1]]); yield
                    act(sq_t[:, cc, :n], pcs[cc][0][:, :n], AF.Square, [pcs[cc][1]], [sq_r]); yield
                mm(p3[:, :n], [(ones_b[:], sq_t[:, cc, :n]) for cc in range(3)], [sq_r, R_const], [p3r]); yield
                act(rs[0][:, :n], p3[:, :n], AF.Ln, [p3r, R_const], [rs[1]], bias=epsc[:, 0:1], scale=1.0 / 384); yield
                act(rs[0][:, :n], rs[0][:, :n], AF.Exp, [rs[1]], [rs[1]], scale=-0.5); yield
                for cc in range(3):
                    stt(qa[:, cc, lt:lt + n], pcs[cc][0][:, :n], cv_("qan", cc), rs[0][:, :n], ALU.mult, ALU.mult,
                        [pcs[cc][1], rs[1], R_const], R_qa(lt, n)); yield
                BKq.give(got[0]); KRSq.give(got[1])

            for blk in G:
                (gt, lt, n, v) = blk
                if v == 0:
                    dma("sp", ropeC[:, :n], C64_d[:, gt:gt + n], [], [R_rope], d_rope)
                    dma("sp", ropeS[:, :n], S64_d[:, gt:gt + n], [], [R_rope], d_rope)
                interleave([chain_q(0, blk), chain_q(1, blk), chain_qa(blk), chain_q(2, blk), chain_q(3, blk)])


            def keyset(gt, v):
                if v == 0:
                    return list(range(18))
                return [18, 19] if gt < NS + 256 else [20, 21]

            def attend(Kt, K_r, kdim, kbase, Qt, Q_r, Vt, V_r, exp_scale, h, recip='dve', ride=None, pend=None, drain=True):
                odd = h % 2
                olo = 64 if odd else 0
                if pend is None:
                    pend = []

                def finish(po, por, lt, n, hf, olo):
                    l_t, l_r = TMP.next()
                    if recip == 'dve':
                        S.op("dve", lambda e: e.reciprocal(l_t[0:64, :n], po[64:128, :n]), reads=[por], writes=[l_r])
                    else:
                        act(l_t[0:64, :n], po[64:128, :n], AF.Ln, [por], [l_r])
                        act(l_t[0:64, :n], l_t[0:64, :n], AF.Exp, [l_r], [l_r], scale=-1.0)
                    return (po, por, l_t, l_r, olo, lt, n, hf)

                for (gt, lt, n, v) in G:
                    po, por = PSA.next()
                    ks = keyset(gt, v)

                    def pv(i, kc, p_t, p_r, po=po, por=por, n=n, nks=len(ks)):
                        lhs = Vt(kc)
                        S.op("pe", lambda e: e.matmul(po[:, :n], lhs, p_t[:, :n], start=(i == 0), stop=(i == nks - 1)),
                             reads=V_r + [p_r], writes=[por])
                    for i, kc in enumerate(ks):
                        pS, pSr = PS.next()
                        mm(pS[:, :n], [(Kt(kc), Qt(lt, n))], K_r + Q_r(lt, n), [pSr])
                        p_t, p_r = PT.next()
                        act(p_t[:, :n], pS[:, :n], AF.Exp, [pSr], [p_r], scale=exp_scale)
                        pend.append((pv, i, kc, p_t, p_r, (po, por, lt, n, h, olo) if i == len(ks) - 1 else None))
                        if len(pend) > 3:
                            f_, i_, kc_, pt_, pr_, fin = pend.pop(0)
                            f_(i_, kc_, pt_, pr_)
                            if fin is not None:
                                yield finish(*fin)
                        if ride is not None and (i % 3 == 2 or (len(ks) < 3 and i == len(ks) - 1)):
                            ride[0] = advance(ride[0], 1)
                while drain and pend:
                    f_, i_, kc_, pt_, pr_, fin = pend.pop(0)
                    f_(i_, kc_, pt_, pr_)
                    if fin is not None:
                        yield finish(*fin)

            cp(KMq[1][64:96, :], KMq[0][64:96, :], R_KMl[0], R_KMl[1])
            S.op("dve", lambda e: e.memset(VM[1][:], 1.0), writes=R_VMl[1])
            if is_s:
                kranges = [(k0, min(512, 2304 - k0)) for k0 in range(0, 2304, 512)]
                vchunks = list(range(18))
            else:
                kranges = [(2304, 512)]
                vchunks = [18, 19, 20, 21]

            def prep_g(h):
                b = h % 2
                for (gt, lt, n, v) in G:
                    pq, pqr = PS.next()
                    mm(pq[0:96, :n], [(wqb[:, kc, h * 96:(h + 1) * 96], qa[:, kc, lt:lt + n]) for kc in range(3)], [R_w] + R_qa(lt, n), [pqr])
                    if is_s:
                        q_t, q_r = TMP.next()
                        cp(q_t[0:96, :n], pq[0:96, :n], [pqr], [q_r])
                        pr_, prr = PS.next()
                        mm(pr_[0:96, :n], [(P96_sb[:], q_t[0:96, :n])], [q_r, R_const], [prr])
                        a_t, a_r = TMP.next()
                        tt(a_t[0:96, :n], q_t[0:96, :n], rC96[:, lt:lt + n], ALU.mult, [q_r, R_r96], [a_r])
                        b_t, b_r = TMP.next()
                        tt(b_t[0:96, :n], pr_[0:96, :n], rS96[:, lt:lt + n], ALU.mult, [prr, R_r96], [b_r])
                        tt(QM[b][:, lt:lt + n], a_t[0:96, :n], b_t[0:96, :n], ALU.add, [a_r, b_r], [R_QM[b]])
                    else:
                        act(QM[b][:, lt:lt + n], pq[0:96, :n], AF.Copy, [pqr], [R_QM[b]])
                    yield
                for (k0, kn_) in kranges:
                    pk, pkr = PS.next()
                    mm(pk[0:64, :kn_], [(wkvb[:, cc, h * 128:h * 128 + 64], ckvT[:, cc, k0:k0 + kn_]) for cc in range(2)], [R_w, R_ckvT], [pkr])
                    act(KMq[b][0:64, k0:k0 + kn_], pk[0:64, :kn_], AF.Copy, [pkr], R_KMl[b])
                    yield
                for i0_ in range(0, len(vchunks), 8):
                    grp = vchunks[i0_:i0_ + 8]
                    pv, pvr = PS.next()
                    for j, kc in enumerate(grp):
                        mm(pv[:, j * 64:(j + 1) * 64], [(ckvT[:, cc, kc * 128:(kc + 1) * 128], wkvb[:, cc, h * 128 + 64:h * 128 + 128]) for cc in range(2)],
                           [R_w, R_ckvT], [pvr])
                    cp(VM[b][:, grp[0]:grp[0] + len(grp), 0:64], pv[:, 0:len(grp) * 64].rearrange("p (j d) -> p j d", d=64), [pvr], R_VMl[b])
                    yield

            pendG = []
            prep0 = [prep_g(0)]
            for h in range(8):
                kvh, half, qc = h // 4, h % 2, h // 2
                for (po, por, l_t, l_r, olo, lt, n, hf) in attend(
                        lambda kc, kvh=kvh: KG[:, kvh, kc * 128:(kc + 1) * 128], [R_KG], 64, 0,
                        lambda lt, n, h=h: QG[:, h, lt:lt + n], R_QG,
                        lambda kc, kvh=kvh: VG[:, kc, kvh, :], [R_VG], 0.125, h, pend=pendG, drain=(h == 7),
                        ride=(prep0 if h == 7 else None)):
                    tt(cat[olo:olo + 64, hf // 2, lt:lt + n], po[0:64, :n], l_t[0:64, :n], ALU.mult,
                       [por, l_r], cat_res(lt, n))
            if prep0[0] is not None:
                run_all(prep0[0])
            pendM = []
            for h in range(8):
                b = h % 2
                nxt = [prep_g(h + 1) if h < 7 else None]
                for (po, por, l_t, l_r, olo, lt, n, hf) in attend(
                        lambda kc, b=b: KMq[b][0:96, kc * 128:(kc + 1) * 128], R_KMl[b], 96, 0,
                        lambda lt, n, b=b: QM[b][0:96, lt:lt + n], lambda lt, n, b=b: [R_QM[b]],
                        lambda kc, b=b: VM[b][:, kc, :], R_VMl[b], 96.0 ** -0.5, h, recip='act', ride=nxt, pend=pendM, drain=(h == 7)):
                    tt(cat[olo:olo + 64, 4 + hf // 2, lt:lt + n], po[0:64, :n], l_t[0:64, :n], ALU.mult,
                       [por, l_r], cat_res(lt, n))
                if nxt[0] is not None:
                    run_all(nxt[0])

            Gm = G if is_s else [(2048, 0, 512, 1)]
            wout_phase(1, awout_d, None, Gm, cat, cat_res, hg, hg_res, None, w_t=QG, w_rl=R_QG(0, 1024))

        for G in (GA, GB, GC):
            group(G)

    attn_all()
    S.barrier()
    ffn_layer(1, [GAf, GBf], PHASE0)
    S.barrier()

    S.emit()
    return nc


_NC_CACHE = {}


def kernel(x_prompt, x_sample, cache_gqa_k, cache_gqa_v, cache_mla_ckv, cache_mla_krope, c, c_ctx,
           ada_w, ada_b, norm_pre, norm_post,
           conv_w_in, conv_sc_w, conv_cf_b_in, conv_cf_dw_w, conv_cf_dw_b, conv_cf_ln_g, conv_cf_ln_b,
           conv_w_out, conv_b_out,
           attn_w_in, attn_q_norm, attn_k_norm, attn_q_a_norm, attn_w_q_b, attn_kv_a_norm, attn_w_kv_b,
           attn_w_out, ffn_w_gate, ffn_w_up, ffn_w_down):
    f = lambda a: np.ascontiguousarray(np.asarray(a, dtype=np.float32))
    if "nc" not in _NC_CACHE:
        _NC_CACHE["nc"] = build_nc()
    nc = _NC_CACHE["nc"]
    C64, S64, P64, C96, S96, P96 = _rope_consts()
    shared = {
        "cctx": f(c_ctx).reshape(8, 128),
        "ada_w": f(ada_w), "ada_b": f(ada_b).reshape(2, 48, 128),
        "norm_pre": f(norm_pre).reshape(32, 128), "norm_post": f(norm_post).reshape(32, 128),
        "conv_w_in": f(conv_w_in)[0], "conv_sc_w": f(conv_sc_w).reshape(12, 128),
        "conv_cf_b_in": f(conv_cf_b_in).reshape(8, 128), "conv_cf_dw_w": f(conv_cf_dw_w).reshape(124, 128),
        "conv_cf_dw_b": f(conv_cf_dw_b).reshape(4, 128), "conv_cf_ln_g": f(conv_cf_ln_g).reshape(4, 128),
        "conv_cf_ln_b": f(conv_cf_ln_b).reshape(4, 128),
        "conv_w_out": f(conv_w_out)[0], "conv_b_out": f(conv_b_out).reshape(8, 128),
        "attn_w_in": f(attn_w_in)[0], "attn_q_norm": f(attn_q_norm).reshape(1, 64), "attn_k_norm": f(attn_k_norm).reshape(1, 64),
        "attn_q_a_norm": f(attn_q_a_norm).reshape(3, 128), "attn_w_q_b": f(attn_w_q_b)[0],
        "attn_kv_a_norm": f(attn_kv_a_norm).reshape(2, 128), "attn_w_kv_b": f(attn_w_kv_b)[0],
        "attn_w_out": f(attn_w_out)[0],
        "ffn_w_gate": f(ffn_w_gate), "ffn_w_up": f(ffn_w_up), "ffn_w_down": f(ffn_w_down),
        "C64": C64, "S64": S64, "P64": P64, "C96": C96, "S96": S96, "P96": P96,
    }
    xp = f(x_prompt); xs = f(x_sample)
    in_maps = []
    for i in range(8):
        m = dict(shared)
        m["xs"] = xs[i]
        m["xp"] = xp[2 * i:2 * i + 2].reshape(NP_, D)
        m["ck"] = f(cache_gqa_k)[i, 0].reshape(256, 128)
        m["cv"] = f(cache_gqa_v)[i, 0].reshape(256, 128)
        m["cckv"] = f(cache_mla_ckv)[i, 0]
        m["ckr"] = f(cache_mla_krope)[i, 0]
        m["crow"] = f(c)[i].reshape(8, 128)
        in_maps.append(m)
    res = run_bass_kernel_spmd(nc, in_maps, core_ids=list(range(8)))
    R = res.results
    y_prompt = np.concatenate([R[i]["yp"].reshape(2, 256, D) for i in range(8)], axis=0)
    y_sample = np.stack([R[i]["ys"] for i in range(8)], axis=0)
    new_k = np.concatenate([R[i]["nk"].reshape(2, 1, 256, 2, 64) for i in range(8)], axis=0)
    new_v = np.concatenate([R[i]["nv"].reshape(2, 1, 256, 2, 64) for i in range(8)], axis=0)
    new_ckv = np.concatenate([R[i]["nckv"].reshape(2, 1, 256, 256) for i in range(8)], axis=0)
    new_kr = np.concatenate([R[i]["nkr"].reshape(2, 1, 256, 32) for i in range(8)], axis=0)
    return (y_prompt.astype(np.float32), y_sample.astype(np.float32), new_k.astype(np.float32),
            new_v.astype(np.float32), new_ckv.astype(np.float32), new_kr.astype(np.float32))
```

```python
import numpy as np
import concourse.bass as bass
import concourse.mybir as mybir
from concourse.bass_utils import run_bass_kernel_spmd

F32 = mybir.dt.float32
BF16 = mybir.dt.bfloat16
AF = mybir.ActivationFunctionType
ALU = mybir.AluOpType

D = 1024
NS = 2048
NP_ = 512
NT = NS + NP_
FF = 2816
EPS = 1e-6
THETA = 10000.0
NK = 256 + NS + NP_
ARENA0 = 16512
ARENA1 = 229344


class Sem:
    def __init__(self, h, is_dma=False):
        self.h = h
        self.val = 0
        self.is_dma = is_dma


class Res:
    __slots__ = ("w", "r", "name")

    def __init__(self, name=""):
        self.w = None
        self.r = {}
        self.name = name


class Sched:
    ENG = ("pe", "act", "dve", "pool", "sp")

    def __init__(self, nc):
        self.nc = nc
        self.q = {k: [] for k in self.ENG}
        self.esem = {k: Sem(nc.alloc_semaphore("sem_" + k)) for k in self.ENG}
        self.dsems = []
        self.waited = {k: {} for k in self.ENG}
        self.all_res = []
        self.nops = 0

    def res(self, name=""):
        r = Res(name)
        self.all_res.append(r)
        return r

    def dsem(self, name):
        name = "%s_%d" % (name, len(self.dsems))
        s = Sem(self.nc.alloc_semaphore(name), True)
        self.dsems.append(s)
        return s

    def op(self, eng, fn, reads=(), writes=(), sem=None, inc=1):
        deps = {}
        own = self.esem[eng]

        def add(tok):
            s, v = tok
            if deps.get(s, 0) < v:
                deps[s] = v

        for R in reads:
            if R.w is not None:
                if not (R.w[0] is own and eng == "pe"):
                    add(R.w)
        for R in writes:
            if R.w is not None and not (R.w[0] is own and eng == "pe"):
                add(R.w)
            for s, v in R.r.items():
                if not (s is own and eng == "pe"):
                    add((s, v))
        waits = []
        wd = self.waited[eng]
        for s, v in deps.items():
            if s.is_dma:
                v = s.val
            if wd.get(s, 0) >= v:
                continue
            wd[s] = v
            waits.append((s, v))
        if sem is None:
            sem = own
        sem.val += inc
        tok = (sem, sem.val)
        for R in reads:
            if R.r.get(sem, 0) < sem.val:
                R.r[sem] = sem.val
        for R in writes:
            R.w = tok
            R.r = {}
        self.q[eng].append((waits, fn, sem, inc))
        self.nops += 1

    def barrier(self):
        sems = list(self.esem.values()) + self.dsems
        for eng in self.ENG:
            waits = []
            wd = self.waited[eng]
            for s in sems:
                if s.val > 0 and wd.get(s, 0) < s.val:
                    wd[s] = s.val
                    waits.append((s, s.val))
            self.q[eng].append((waits, None, None, 0))
        for r in self.all_res:
            r.w = None
            r.r = {}

    def emit(self):
        nc = self.nc
        handles = {"pe": "tensor", "act": "scalar", "dve": "vector", "pool": "gpsimd", "sp": "sync"}
        self.barrier()
        with nc.Block() as block:
            for k in self.ENG:
                def body(e, k=k):
                    for waits, fn, sem, inc in self.q[k]:
                        for s, v in waits:
                            e.wait_ge(s.h, v)
                        if fn is None:
                            continue
                        inst = fn(e)
                        inst.then_inc(sem.h, inc)
                getattr(block, handles[k])(body)


class Rot:
    def __init__(self, items):
        self.items = items
        self.i = 0

    def next(self):
        it = self.items[self.i % len(self.items)]
        self.i += 1
        return it


def _rope_consts():
    t = np.arange(NS)
    row = (t // 64).astype(np.float64)
    col = (t % 64).astype(np.float64)
    C64 = np.ones((128, NS), np.float64)
    S64 = np.zeros((128, NS), np.float64)
    P64 = np.zeros((128, 128), np.float32)
    for p in range(128):
        d = p % 64
        pos = row if d < 32 else col
        i = d % 16
        inv = THETA ** (-(2.0 * i) / 32.0)
        ang = pos * np.float64(np.float32(inv))
        first = (d % 32) < 16
        C64[p] = np.cos(ang)
        S64[p] = -np.sin(ang) if first else np.sin(ang)
        partner = p + 16 if first else p - 16
        P64[partner, p] = 1.0
    C96 = np.ones((96, NS), np.float64)
    S96 = np.zeros((96, NS), np.float64)
    P96 = np.zeros((96, 96), np.float32)
    for p in range(64, 96):
        d = p - 64
        pos = row if d < 16 else col
        i = d % 8
        inv = THETA ** (-(2.0 * i) / 16.0)
        ang = pos * np.float64(np.float32(inv))
        first = (d % 16) < 8
        C96[p] = np.cos(ang)
        S96[p] = -np.sin(ang) if first else np.sin(ang)
        partner = p + 8 if first else p - 8
        P96[partner, p] = 1.0
    return (C64.astype(np.float32), S64.astype(np.float32), P64,
            C96.astype(np.float32), S96.astype(np.float32), P96)


LAST_NAMES = {}
LAST_SCHED = [None]
DEBUG_SKIP = set()

def build_nc(debug=None, stop=None):
    nc = bass.Bass("TRN2", target_bir_lowering=False)
    S = Sched(nc)
    LAST_SCHED[0] = S

    def din(name, shape):
        return nc.dram_tensor(name, list(shape), F32, kind="ExternalInput").ap()

    def dout(name, shape):
        return nc.dram_tensor(name, list(shape), F32, kind="ExternalOutput").ap()

    xs_d = din("xs", [NS, D]); xp_d = din("xp", [NP_, D])
    ck_d = din("ck", [256, 128]); cv_d = din("cv", [256, 128])
    cckv_d = din("cckv", [256, 256]); ckr_d = din("ckr", [256, 32])
    crow_d = din("crow", [8, 128]); cctx_d = din("cctx", [8, 128])
    ada_w_d = din("ada_w", [2, D, 6 * D]); ada_b_d = din("ada_b", [2, 48, 128])
    npre_d = din("norm_pre", [32, 128]); npost_d = din("norm_post", [32, 128])
    cwin_d = din("conv_w_in", [D, 2560]); scw_d = din("conv_sc_w", [12, 128])
    cfbin_d = din("conv_cf_b_in", [8, 128]); dww_d = din("conv_cf_dw_w", [124, 128])
    dwb_d = din("conv_cf_dw_b", [4, 128]); lng_d = din("conv_cf_ln_g", [4, 128]); lnb_d = din("conv_cf_ln_b", [4, 128])
    cwout_d = din("conv_w_out", [D, D]); cbout_d = din("conv_b_out", [8, 128])
    awin_d = din("attn_w_in", [D, 1440]); qn_d = din("attn_q_norm", [1, 64]); kn_d = din("attn_k_norm", [1, 64])
    qan_d = din("attn_q_a_norm", [3, 128]); wqb_d = din("attn_w_q_b", [384, 768])
    kvan_d = din("attn_kv_a_norm", [2, 128]); wkvb_d = din("attn_w_kv_b", [256, 1024])
    awout_d = din("attn_w_out", [D, D])
    wg_d = din("ffn_w_gate", [2, D, FF]); wu_d = din("ffn_w_up", [2, D, FF]); wd_d = din("ffn_w_down", [2, FF, D])
    C64_d = din("C64", [128, NS]); S64_d = din("S64", [128, NS]); P64_d = din("P64", [128, 128])
    C96_d = din("C96", [96, NS]); S96_d = din("S96", [96, NS]); P96_d = din("P96", [96, 96])

    ys_d = dout("ys", [NS, D]); yp_d = dout("yp", [NP_, D])
    nk_d = dout("nk", [NP_, 128]); nv_d = dout("nv", [NP_, 128])
    nckv_d = dout("nckv", [NP_, 256]); nkr_d = dout("nkr", [NP_, 32])
    dbg_d = {}
    if debug:
        for name, shape in debug.items():
            dbg_d[name] = dout("dbg_" + name, shape)

    off = [ARENA0]
    sbn = [0]

    def sb(name, shape, dt, at=None):
        sbn[0] += 1
        name = "%s_%d" % (name, sbn[0])
        n = int(np.prod(shape[1:])) * (4 if dt == F32 else 2)
        n = (n + 63) // 64 * 64
        o = off[0] if at is None else at
        t = nc.alloc_sbuf_tensor_at(name, list(shape), dt, offset=o)
        LAST_NAMES[name.rsplit("_", 1)[0]] = t.name
        if at is None:
            off[0] += n
            assert off[0] <= ARENA1, (name, off[0])
        else:
            assert o + n <= ARENA1, (name, o + n)
        return t

    x_scr = nc.dram_tensor("x_scr", [128, 8, NT], F32).ap()
    h_scr = nc.dram_tensor("h_scr", [128, 8, NT], BF16).ap()
    if stop is not None:
        xdump_d = dout("xdump", [128, 8, NT])
        d_dump = S.dsem("d_dump")
        _emit = S.emit

        def emit_with_dump():
            S.barrier()
            for b5 in range(NT // 512 if "dump" not in DEBUG_SKIP else 0):
                xb, xb_r, d_xl, d_xs = XBLK.next()
                S.op("sp", lambda e, b5=b5, xb=xb: e.dma_start(out=xb[:], in_=x_scr[:, :, b5 * 512:(b5 + 1) * 512]),
                     reads=rx(b5 * 512, 512), writes=[xb_r], sem=d_xl, inc=16)
                S.op("sp", lambda e, b5=b5, xb=xb: e.dma_start(out=xdump_d[:, :, b5 * 512:(b5 + 1) * 512], in_=xb[:]),
                     reads=[xb_r], sem=d_dump, inc=16)
            _emit()
        S.emit = emit_with_dump
    RX = [S.res("x%d" % i) for i in range(NT // 256)]

    def rx(t0, n):
        return RX[t0 // 256:(t0 + n + 255) // 256]

    xblk = [sb("xblk%d" % i, [128, 8, 512], F32) for i in range(2)]
    XBLK = Rot([(xblk[i], S.res("xblk%d" % i), S.dsem("d_xl%d" % i), S.dsem("d_xs%d" % i)) for i in range(2)])

    ident_f = sb("ident_f", [128, 128], F32)
    ident_b = sb("ident_b", [128, 128], BF16)
    ones_b = sb("ones_b", [128, 128], BF16)
    blk64_b = sb("blk64_b", [128, 128], BF16)
    P64_sb = sb("P64_sb", [128, 128], F32)
    P96_sb = sb("P96_sb", [96, 96], F32)
    epsc = sb("epsc", [128, 1], F32)
    NCOL = 384
    colv = sb("colv", [128, NCOL], F32)
    cs_b = sb("cs_b", [128, 8, 2], BF16)
    mod = sb("mod", [128, 2, 48, 2], F32)
    Acoef = sb("Acoef", [128, 2, 2, 8, 2], F32)
    Gcoef = sb("Gcoef", [128, 2, 2, 8, 2], F32)
    R_const = S.res("const")
    R_mod = S.res("mod")

    tmpf = [sb("tmpf%d" % i, [128, 512], F32) for i in range(5)]
    TMP = Rot([(tmpf[i], S.res("tmpf%d" % i)) for i in range(5)])
    tmpb = [sb("tmpb%d" % i, [128, 512], BF16) for i in range(4)]
    TMPB = Rot([(tmpb[i], S.res("tmpb%d" % i)) for i in range(4)])
    sqb = [sb("sqb%d" % i, [128, 4, 512], BF16) for i in range(2)]
    SQB = Rot([(sqb[i], S.res("sqb%d" % i)) for i in range(2)])
    rsb = [sb("rs%d" % i, [128, 512], F32) for i in range(3)]
    RS = Rot([(rsb[i], S.res("rs%d" % i)) for i in range(3)])

    ps = [nc.alloc_psum_tensor("ps%d" % i, [128, 512], F32) for i in range(8)]
    PS_RES = [S.res("ps%d" % i) for i in range(8)]
    PS = Rot([(ps[i], PS_RES[i]) for i in range(6)])
    PSA = Rot([(ps[i], PS_RES[i]) for i in range(6, 8)])
    PSN = Rot([(ps[6], PS_RES[6])])


    def mm(out, pairs, reads, writes):
        def fn(e):
            inst = None
            n = len(pairs)
            for i, (l, r) in enumerate(pairs):
                inst = e.matmul(out, l, r, start=(i == 0), stop=(i == n - 1))
            return inst
        S.op("pe", fn, reads=reads, writes=writes)

    def tr(out, in_, idn, reads, writes):
        S.op("pe", lambda e: e.transpose(out, in_, idn), reads=reads, writes=writes)

    def act(out, in_, func, reads, writes, bias=None, scale=None):
        kw = {}
        if bias is not None:
            kw["bias"] = bias
        if scale is not None:
            kw["scale"] = scale
        S.op("act", lambda e: e.activation(out, in_, func, **kw), reads=reads, writes=writes)

    def tt(out, a, b, op, reads, writes, eng="dve"):
        S.op(eng, lambda e: e.tensor_tensor(out, a, b, op), reads=reads, writes=writes)

    def stt(out, in0, scalar, in1, op0, op1, reads, writes):
        S.op("dve", lambda e: e.scalar_tensor_tensor(out, in0, scalar, in1, op0, op1), reads=reads, writes=writes)

    def ts(out, in0, s1, s2, op0, op1, reads, writes, eng="dve"):
        if op1 is None:
            S.op(eng, lambda e: e.tensor_scalar(out, in0, s1, None, op0), reads=reads, writes=writes)
        else:
            S.op(eng, lambda e: e.tensor_scalar(out, in0, s1, s2, op0, op1), reads=reads, writes=writes)

    def cp(out, in_, reads, writes, eng="dve"):
        S.op(eng, lambda e: e.tensor_copy(out, in_), reads=reads, writes=writes)

    def dma(q, out, in_, reads, writes, sem, **kw):
        S.op(q, lambda e: e.dma_start(out=out, in_=in_, **kw), reads=reads, writes=writes, sem=sem, inc=16)

    def rstd_from_ps(pst, n, inv_d, reads_extra=()):
        (pt, pr) = pst
        rs_t, rs_r = RS.next()
        act(rs_t[:, :n], pt[:, :n], AF.Ln, reads=[pr, R_const], writes=[rs_r], bias=epsc[:, 0:1], scale=inv_d)
        act(rs_t[:, :n], rs_t[:, :n], AF.Exp, reads=[rs_r], writes=[rs_r], scale=-0.5)
        return rs_t, rs_r

    d_c = S.dsem("d_const")
    stg = [sb("stg%d" % i, [128, 128], F32) for i in range(3)]
    R_stg = [S.res("stg%d" % i) for i in range(3)]
    iot = sb("iot", [128, 128], F32)
    R_iot = S.res("iot")
    PHASE0 = off[0]
    col_of = {}
    rows = []

    def add_rows(name, ap, n):
        col_of[name] = len(rows)
        for i in range(n):
            rows.append((ap, i))

    add_rows("ada_b0", ada_b_d[0], 48); add_rows("ada_b1", ada_b_d[1], 48)
    add_rows("npre", npre_d, 32); add_rows("npost", npost_d, 32)
    add_rows("cfbin", cfbin_d, 8); add_rows("dwb", dwb_d, 4); add_rows("lng", lng_d, 4); add_rows("lnb", lnb_d, 4)
    add_rows("cbout", cbout_d, 8); add_rows("scw", scw_d, 12); add_rows("dww", dww_d, 124)
    add_rows("qan", qan_d, 3); add_rows("kvan", kvan_d, 2); add_rows("crow", crow_d, 8); add_rows("cctx", cctx_d, 8)
    col_of["qn"] = len(rows); rows.append(("qn", 0))
    col_of["kn"] = len(rows); rows.append(("kn", 0))
    assert len(rows) <= NCOL, len(rows)
    for i in range(3):
        S.op("dve", lambda e, i=i: e.memset(stg[i][:], 0.0), writes=[R_stg[i]])
    r = 0
    while r < len(rows):
        ap, i0 = rows[r]
        if isinstance(ap, str):
            src = qn_d if ap == "qn" else kn_d
            si, p = divmod(r, 128)
            dma("sp", stg[si][p:p + 1, 0:64], src, [], [R_stg[si]], d_c)
            dma("sp", stg[si][p:p + 1, 64:128], src, [], [R_stg[si]], d_c)
            r += 1
            continue
        n = 1
        while (r + n < len(rows) and rows[r + n][0] is ap and rows[r + n][1] == i0 + n and (r + n) // 128 == r // 128):
            n += 1
        si, p = divmod(r, 128)
        dma("sp", stg[si][p:p + n, :], ap[i0:i0 + n, :], [], [R_stg[si]], d_c)
        r += n
    dma("sp", P64_sb[:], P64_d, [], [R_const], d_c)
    dma("sp", P96_sb[:], P96_d, [], [R_const], d_c)
    S.op("pool", lambda e: e.iota(iot[:], [[1, 128]], base=0, channel_multiplier=-1,
                                   allow_small_or_imprecise_dtypes=True), writes=[R_iot])
    ts(ident_f[:], iot[:], 0.0, None, ALU.is_equal, None, [R_iot], [R_const])
    ts(ident_b[:], iot[:], 0.0, None, ALU.is_equal, None, [R_iot], [R_const])
    S.op("dve", lambda e: e.memset(ones_b[:], 1.0), writes=[R_const])
    S.op("dve", lambda e: e.memset(blk64_b[:], 0.0), writes=[R_const])
    S.op("dve", lambda e: e.memset(blk64_b[0:64, 0:64], 1.0), writes=[R_const])
    S.op("dve", lambda e: e.memset(blk64_b[64:128, 64:128], 1.0), writes=[R_const])
    S.op("dve", lambda e: e.memset(epsc[:], EPS), writes=[R_const])
    for si in range(3):
        pt, pr = PS.next()
        tr(pt[:, 0:128], stg[si][:], ident_f[:], [R_stg[si], R_const], [pr])
        cp(colv[:, si * 128:(si + 1) * 128], pt[:, 0:128], [pr], [R_const])

    def cv_(name, i=0):
        c = col_of[name] + i
        return colv[:, c:c + 1]

    act(cs_b[:, :, 0], colv[:, col_of["crow"]:col_of["crow"] + 8], AF.Silu, [R_const], [R_const])
    act(cs_b[:, :, 1], colv[:, col_of["cctx"]:col_of["cctx"] + 8], AF.Silu, [R_const], [R_const])

    d_aw = [S.dsem("d_aw0"), S.dsem("d_aw1")]
    R_modl = [S.res("mod0"), S.res("mod1")]

    def ada_g(l, awb, R_awb, mpt, mpr, stage=None):
        src_l = ada_w_d[l].rearrange("(kc p) n -> p kc n", p=128)
        for ng in range(12):
            b = ng % 2
            if stage is None:
                dma("pool", awb[b][:], src_l[:, :, ng * 512:(ng + 1) * 512], [], [R_awb[b]], d_aw[b])
            else:
                afw, R_afw, d_af = stage
                dma("sp", afw[b][:], src_l[:, :, ng * 512:(ng + 1) * 512], [], [R_afw[b]], d_af[b])
                cp(awb[b][:, 0:4, :], afw[b][:, 0:4, :], [R_afw[b]], [R_awb[b]])
                act(awb[b][:, 4:8, :], afw[b][:, 4:8, :], AF.Copy, [R_afw[b]], [R_awb[b]])
            for j in range(4):
                nchunk = ng * 4 + j
                mm(mpt[:, nchunk * 2:nchunk * 2 + 2],
                   [(awb[b][:, kc, j * 128:(j + 1) * 128], cs_b[:, kc, :]) for kc in range(8)],
                   [R_awb[b], R_const], [mpr])
            yield
        name = "ada_b%d" % l
        for v in range(2):
            tt(mod[:, l, :, v], mpt[:, 0:96].rearrange("p (j v) -> p j v", v=2)[:, :, v],
               colv[:, col_of[name]:col_of[name] + 48], ALU.add, [mpr, R_const], [R_modl[l]])
        for s_ in range(2):
            for v in range(2):
                npre_c = colv[:, col_of["npre"] + (l * 2 + s_) * 8: col_of["npre"] + (l * 2 + s_) * 8 + 8]
                npost_c = colv[:, col_of["npost"] + (l * 2 + s_) * 8: col_of["npost"] + (l * 2 + s_) * 8 + 8]
                S.op("dve", lambda e, l=l, s_=s_, v=v, npre_c=npre_c: e.scalar_tensor_tensor(
                    Acoef[:, l, s_, :, v], mod[:, l, (3 * s_ + 1) * 8:(3 * s_ + 2) * 8, v], 1.0, npre_c, ALU.add, ALU.mult),
                    reads=[R_modl[l], R_const], writes=[R_modl[l]])
                tt(Gcoef[:, l, s_, :, v], mod[:, l, (3 * s_ + 2) * 8:(3 * s_ + 3) * 8, v], npost_c, ALU.mult,
                   [R_modl[l], R_const], [R_modl[l]])

    awb0 = [sb("awb%d" % i, [128, 8, 512], BF16, at=PHASE0 + 8192 + i * 8192) for i in range(2)]
    afw0 = [sb("afw%d" % i, [128, 8, 512], F32, at=PHASE0 + 24576 + i * 16384) for i in range(2)]
    stage0 = (afw0, [S.res("afw0"), S.res("afw1")], [S.dsem("d_af0"), S.dsem("d_af1")])

    def A_(l, s_, c, v):
        return Acoef[:, l, s_, c, v:v + 1]

    def B_(l, s_, c, v):
        return mod[:, l, 3 * s_ * 8 + c, v:v + 1]

    def G_(l, s_, c, v):
        return Gcoef[:, l, s_, c, v:v + 1]

    def run_all(g):
        for _ in g:
            pass

    def advance(g, k):
        if g is None:
            return None
        for _ in range(k):
            try:
                next(g)
            except StopIteration:
                return None
        return g

    run_all(ada_g(0, awb0, [S.res("awb0"), S.res("awb1")], ps[7], PS_RES[7], stage=stage0))

    S.barrier()
    if stop == "setup":
        S.emit()
        return nc

    def prenorm_g(l, s_, blocks, dst, dst_res, from_input=None):
        pending_store = []
        for (gt, lt, n, v) in blocks:
            xb, xb_r, d_xl, d_xs = XBLK.next()
            if from_input is None:
                dma("sp", xb[:, :, :n], x_scr[:, :, gt:gt + n], rx(gt, n), [xb_r], d_xl)
            else:
                xin, R_xin2, d_xin2 = from_input
                for j4 in range(n // 128):
                    tcid = gt // 128 + j4
                    rows = xs_d[tcid * 128:(tcid + 1) * 128, :] if tcid < NS // 128 else xp_d[(tcid - NS // 128) * 128:(tcid - NS // 128 + 1) * 128, :]
                    for half in range(2):
                        dma("sp", xin[:, half * 512:(half + 1) * 512], rows[:, half * 512:(half + 1) * 512], [], [R_xin2[half]], d_xin2[half])
                    if j4 == 0 and pending_store:
                        pending_store.pop()()
                    for half in range(2):
                        pt, pr = PS.next()
                        for j in range(4):
                            c = half * 4 + j
                            tr(pt[:, j * 128:(j + 1) * 128], xin[:, c * 128:(c + 1) * 128], ident_f[:], [R_xin2[half], R_const], [pr])
                        dstx = xb[:, half * 4:(half + 1) * 4, j4 * 128:(j4 + 1) * 128]
                        srcp = pt[:, :].rearrange("p (j t) -> p j t", j=4)
                        if half == 0:
                            cp(dstx, srcp, [pr], [xb_r])
                        else:
                            act(dstx, srcp, AF.Copy, [pr], [xb_r])
                    yield
                pending_store.append(lambda xb=xb, xb_r=xb_r, gt=gt, n=n, d_xs=d_xs:
                                     dma("sp", x_scr[:, :, gt:gt + n], xb[:, :, :n], [xb_r], rx(gt, n), d_xs))
            pst = PSN.next()
            for c in range(8):
                sq_t, sq_r = TMPB.next()
                act(sq_t[:, :n], xb[:, c, :n], AF.Square, [xb_r], [sq_r])
                S.op("pe", lambda e, c=c, sq_t=sq_t, n=n, pst=pst: e.matmul(pst[0][:, :n], ones_b[:], sq_t[:, :n], start=(c == 0), stop=(c == 7)),
                     reads=[sq_r, R_const], writes=[pst[1]])
                yield
            rs_t, rs_r = rstd_from_ps(pst, n, 1.0 / D)
            yield
            for c in range(8):
                t_t, t_r = TMP.next()
                stt(t_t[:, :n], xb[:, c, :n], A_(l, s_, c, v), rs_t[:, :n], ALU.mult, ALU.mult,
                    [xb_r, rs_r, R_modl[l]], [t_r])
                act(dst[:, c, lt:lt + n], t_t[:, :n], AF.Identity, [t_r, R_modl[l]], dst_res(lt, n), bias=B_(l, s_, c, v), scale=1.0)
                yield
        if pending_store:
            pending_store.pop()()

    def postnorm_g(l, s_, blocks, o, o_res, final=None):
        xbs = {}

        def issue_load(i):
            (gt_, lt_, n_, v_) = blocks[i]
            xbi = XBLK.next()
            dma("sp", xbi[0][:, :, :n_], x_scr[:, :, gt_:gt_ + n_], rx(gt_, n_), [xbi[1]], xbi[2])
            xbs[i] = xbi
        for i in range(min(2, len(blocks))):
            issue_load(i)
        yield
        for bi_, (gt, lt, n, v) in enumerate(blocks):
            xb, xb_r, d_xl, d_xs = xbs[bi_]
            pst = PSN.next()
            for c in range(8):
                sq_t, sq_r = TMPB.next()
                act(sq_t[:, :n], o[:, c, lt:lt + n], AF.Square, o_res(lt, n), [sq_r])
                S.op("pe", lambda e, c=c, sq_t=sq_t, n=n, pst=pst: e.matmul(pst[0][:, :n], ones_b[:], sq_t[:, :n], start=(c == 0), stop=(c == 7)),
                     reads=[sq_r, R_const], writes=[pst[1]])
                yield
            rs_t, rs_r = rstd_from_ps(pst, n, 1.0 / D)
            yield
            for c in range(8):
                t_t, t_r = TMP.next()
                stt(t_t[:, :n], o[:, c, lt:lt + n], G_(l, s_, c, v), rs_t[:, :n], ALU.mult, ALU.mult,
                    o_res(lt, n) + [rs_r, R_modl[l]], [t_r])
                tt(xb[:, c, :n], xb[:, c, :n], t_t[:, :n], ALU.add, [xb_r, t_r], [xb_r])
                yield
            if final is None:
                dma("sp", x_scr[:, :, gt:gt + n], xb[:, :, :n], [xb_r], rx(gt, n), d_xs)
                if bi_ + 2 < len(blocks):
                    issue_load(bi_ + 2)
                continue
            yst, R_yst, d_y = final
            for j4 in range(n // 128):
                tcid = gt // 128 + j4
                b = tcid % 2
                for half in range(2):
                    pt, pr = PS.next()
                    for j in range(4):
                        c = half * 4 + j
                        tr(pt[:, j * 128:(j + 1) * 128], xb[:, c, j4 * 128:(j4 + 1) * 128], ident_f[:], [xb_r, R_const], [pr])
                    if half == 0:
                        cp(yst[b][:, 0:512], pt[:, :], [pr], [R_yst[b]])
                    else:
                        act(yst[b][:, 512:1024], pt[:, :], AF.Copy, [pr], [R_yst[b]])
                dst = ys_d[tcid * 128:(tcid + 1) * 128, :] if tcid < NS // 128 else yp_d[(tcid - NS // 128) * 128:(tcid - NS // 128 + 1) * 128, :]
                dma("sp", dst, yst[b][:], [R_yst[b]], [], d_y[b])
                yield
            if bi_ + 2 < len(blocks):
                issue_load(bi_ + 2)

    def prenorm(l, s_, blocks, dst, dst_res):
        run_all(prenorm_g(l, s_, blocks, dst, dst_res))

    def postnorm_res(l, s_, blocks, o, o_res):
        run_all(postnorm_g(l, s_, blocks, o, o_res))

    class Alloc:
        def __init__(self, items):
            self.free = list(items)

        def take(self, k):
            out, self.free = self.free[:k], self.free[k:]
            return out

        def give(self, items):
            self.free.extend(items)

    def acquire(*needs):
        while True:
            if all(len(a.free) >= k for a, k in needs):
                return [a.take(k) for a, k in needs]
            yield

    def interleave(gens):
        gens = list(gens)
        while gens:
            for g_ in list(gens):
                try:
                    next(g_)
                except StopIteration:
                    gens.remove(g_)

    def make_res(nblk, name):
        rr = [S.res("%s%d" % (name, i)) for i in range(nblk)]

        def f(t0, n):
            return rr[t0 // 256:(t0 + n + 255) // 256]
        return f

    d_wgu = [S.dsem("d_wgu%d" % i) for i in range(3)]
    d_wdb = [S.dsem("d_wdb%d" % i) for i in range(2)]
    d_wst = [S.dsem("d_wst%d" % i) for i in range(2)]
    d_wo = S.dsem("d_wo")

    def wout_phase(l, w_d, bias_name, blocks, cat, cat_res, o, o_res, at, w_t=None, w_rl=None):
        if w_t is None:
            w_t = sb("wout_t", [128, 8, D], BF16, at=at)
            w_rl = [S.res("wout")]
        src = w_d.rearrange("(kc p) n -> p kc n", p=128)
        for h2 in range(2):
            dma("pool", w_t[:, :, h2 * 512:(h2 + 1) * 512], src[:, :, h2 * 512:(h2 + 1) * 512], [], w_rl, d_wo)
        post = None
        for blk in blocks:
            (gt, lt, n, v) = blk
            post_next = postnorm_g(l, 0, [blk], o, o_res)
            next(post_next)
            for nch in range(8):
                pt, pr = PS.next()
                mm(pt[:, :n], [(w_t[:, kc, nch * 128:(nch + 1) * 128], cat[:, kc, lt:lt + n]) for kc in range(8)],
                   w_rl + cat_res(lt, n), [pr])
                if bias_name is not None:
                    act(o[:, nch, lt:lt + n], pt[:, :n], AF.Identity, [pr, R_const], o_res(lt, n), bias=cv_(bias_name, nch), scale=1.0)
                else:
                    act(o[:, nch, lt:lt + n], pt[:, :n], AF.Copy, [pr], o_res(lt, n))
                post = advance(post, 3)
            if post is not None:
                run_all(post)
            post = post_next
        run_all(post)

    def ffn_layer(l, groups, base):
        o0 = base
        hfs = [sb("hf%d" % i, [128, 8, 1280], BF16, at=o0 + i * 20480) for i in range(2)]; o0 += 2 * 20480
        actb = sb("actb", [128, 22, 1280], BF16, at=o0); o0 += 56320
        wgu = [sb("wgu%d" % i, [128, 8, 256], BF16, at=o0 + i * 4096) for i in range(3)]; o0 += 3 * 4096
        wdb = [sb("wdb%d" % i, [128, 22, 128], BF16, at=o0 + i * 5632) for i in range(2)]; o0 += 2 * 5632
        ada = None
        final = None
        if l == 0:
            awb1 = [sb("awbx%d" % i, [128, 8, 512], BF16, at=o0 + i * 8192) for i in range(2)]; o0 += 2 * 8192
            ada = ada_g(1, awb1, [S.res("awbx0"), S.res("awbx1")], ps[7], PS_RES[7])
        else:
            yst = [sb("yst%d" % i, [128, D], F32, at=o0 + i * 4096) for i in range(2)]; o0 += 2 * 4096
            final = (yst, [S.res("yst0"), S.res("yst1")], [S.dsem("d_y0"), S.dsem("d_y1")])
        assert o0 <= ARENA1, o0
        hf_ress = [make_res(5, "hfa"), make_res(5, "hfb")]
        act_res = make_res(5, "actb")
        R_wgu = [S.res("wgu%d" % i) for i in range(3)]
        R_wdb = [S.res("wdb%d" % i) for i in range(2)]
        srcg = wg_d[l].rearrange("(kc p) n -> p kc n", p=128)
        srcu = wu_d[l].rearrange("(kc p) n -> p kc n", p=128)
        srcd = wd_d[l].rearrange("(hc p) n -> p hc n", p=128)
        wi = [0, 0]
        run_all(prenorm_g(l, 1, groups[0], hfs[0], hf_ress[0]))
        post = None
        for gi, blocks in enumerate(groups):
            hf = hfs[gi % 2]; hf_res = hf_ress[gi % 2]
            for hc in range(22):
                b = wi[0] % 3; wi[0] += 1
                dma("pool", wgu[b][:, :, 0:128], srcg[:, :, hc * 128:(hc + 1) * 128], [], [R_wgu[b]], d_wgu[b])
                dma("pool", wgu[b][:, :, 128:256], srcu[:, :, hc * 128:(hc + 1) * 128], [], [R_wgu[b]], d_wgu[b])
                for (gt, lt, n, v) in blocks:
                    pg, pgr = PS.next()
                    pu, pur = PS.next()
                    mm(pg[:, :n], [(wgu[b][:, kc, 0:128], hf[:, kc, lt:lt + n]) for kc in range(8)], [R_wgu[b]] + hf_res(lt, n), [pgr])
                    mm(pu[:, :n], [(wgu[b][:, kc, 128:256], hf[:, kc, lt:lt + n]) for kc in range(8)], [R_wgu[b]] + hf_res(lt, n), [pur])
                    t_t, t_r = TMP.next()
                    act(t_t[:, :n], pg[:, :n], AF.Silu, [pgr], [t_r])
                    tt(actb[:, hc, lt:lt + n], pu[:, :n], t_t[:, :n], ALU.mult, [pur, t_r], act_res(lt, n))
                    post = advance(post, 1)
            run_all(post) if post is not None else None
            post = None
            pre = None
            post_last = None
            if gi + 1 < len(groups):
                pre = prenorm_g(l, 1, groups[gi + 1], hfs[(gi + 1) % 2], hf_ress[(gi + 1) % 2])
            else:
                post_last = postnorm_g(l, 1, blocks, hf, hf_res, final=final)
                next(post_last)
            for nch in range(8):
                b = wi[1] % 2; wi[1] += 1
                dma("pool", wdb[b][:], srcd[:, :, nch * 128:(nch + 1) * 128], [], [R_wdb[b]], d_wdb[b])
                for (gt, lt, n, v) in blocks:
                    pt, pr = PS.next()
                    mm(pt[:, :n], [(wdb[b][:, hc, :], actb[:, hc, lt:lt + n]) for hc in range(22)], [R_wdb[b]] + act_res(lt, n), [pr])
                    act(hf[:, nch, lt:lt + n], pt[:, :n], AF.Copy, [pr], hf_res(lt, n))
                    pre = advance(pre, 3)
                ada = advance(ada, 1)
            run_all(pre) if pre is not None else None
            post = post_last if post_last is not None else postnorm_g(l, 1, blocks, hf, hf_res, final=final)
        run_all(post)
        if ada is not None:
            run_all(ada)

    GA = [(0, 0, 512, 0), (512, 512, 512, 0)]
    GB = [(1024, 0, 512, 0), (1536, 512, 512, 0)]
    GC = [(2048, 0, 256, 1), (2304, 256, 256, 1)]

    def conv_layer(blocks, ntok, pads31, pads3):
        o0 = PHASE0
        h = sb("h0", [128, 8, NT], BF16, at=o0); o0 += 8 * NT * 2
        cat = sb("cat0", [128, 8, NT], BF16, at=o0); o0 += 8 * NT * 2
        zpad = sb("zpad", [128, 2624], BF16, at=o0); o0 += 5248
        ppad = sb("ppad", [128, 2624], BF16, at=o0); o0 += 5248
        gb = sb("gb", [128, NT], F32, at=o0); o0 += NT * 4
        dg = sb("dg", [128, 34, 128], BF16, at=o0); o0 += 34 * 256
        wst = [sb("wst%d" % i, [128, 8, 384], BF16, at=o0 + i * 6144) for i in range(2)]; o0 += 2 * 6144
        wout_at = o0; o0 += 16384
        xin_c = sb("xin_c", [128, D], F32, at=o0); o0 += 4096
        assert o0 <= ARENA1, o0
        h_res = make_res(10, "h0"); cat_res = make_res(10, "cat0"); catB_res = make_res(10, "cat0B")
        R_zpad = S.res("zpad"); R_ppad = S.res("ppad"); R_gb = S.res("gb"); R_dg = S.res("dg")
        R_wst = [S.res("wst0"), S.res("wst1")]
        S.op("dve", lambda e: e.memset(zpad[:], 0.0), writes=[R_zpad])
        S.op("dve", lambda e: e.memset(ppad[:], 0.0), writes=[R_ppad])
        pre = prenorm_g(0, 0, blocks, h, h_res, from_input=(xin_c, [S.res("xin_c0"), S.res("xin_c1")], [S.dsem("d_xin0"), S.dsem("d_xin1")]))
        nst = [n // 128 + 17 for (_, _, n, _) in blocks]
        srcw = cwin_d.rearrange("(kc p) n -> p kc n", p=128)
        wi = 0

        def b_in(ci, b, bi):
            (gt, lt, n, v) = blocks[bi]
            pa, par = PS.next()
            pg, pgr = PS.next()
            mm(pa[:, :n], [(wst[b][:, kc, 0:128], h[:, kc, lt:lt + n]) for kc in range(8)], [R_wst[b]] + h_res(lt, n), [par])
            mm(pg[:, :n], [(wst[b][:, kc, 128:256], h[:, kc, lt:lt + n]) for kc in range(8)], [R_wst[b]] + h_res(lt, n), [pgr])
            t_t, t_r = TMP.next()
            act(t_t[:, :n], pg[:, :n], AF.Sigmoid, [pgr, R_const], [t_r], bias=cv_("cfbin", 4 + ci), scale=1.0)
            stt(zpad[:, pads31[bi]:pads31[bi] + n], pa[:, :n], cv_("cfbin", ci), t_t[:, :n], ALU.add, ALU.mult,
                [par, t_r, R_const], [R_zpad])

        def b_dw(ci, bi):
            (gt, lt, n, v) = blocks[bi]
            pt, pr = PS.next()
            p0 = pads31[bi] - 15
            mm(pt[:, :n], [(dg[:, j, :], zpad[:, p0 + j:p0 + j + n]) for j in range(31)], [R_dg, R_zpad], [pr])
            act(cat[:, 4 + ci, lt:lt + n], pt[:, :n], AF.Identity, [pr, R_const], catB_res(lt, n), bias=cv_("dwb", ci), scale=1.0)

        nb = len(blocks)
        for ci in range(4):
            b = wi % 2; wi += 1
            dma("pool", wst[b][:, :, 0:128], srcw[:, :, 1536 + ci * 128:1536 + (ci + 1) * 128], [], [R_wst[b]], d_wst[b])
            dma("pool", wst[b][:, :, 128:256], srcw[:, :, 2048 + ci * 128:2048 + (ci + 1) * 128], [], [R_wst[b]], d_wst[b])
            for j in range(31):
                ts(dg[:, j, :], ident_b[:], cv_("dww", j * 4 + ci), None, ALU.mult, None, [R_const], [R_dg])
            if ci == 0:
                pre = advance(pre, nst[0])
                for bi in range(nb):
                    if bi + 1 < nb:
                        pre = advance(pre, nst[bi + 1] - 17)
                        b_in(ci, b, bi)
                        pre = advance(pre, 9)
                        if bi >= 1:
                            b_dw(ci, bi - 1)
                        pre = advance(pre, 8)
                    else:
                        b_in(ci, b, bi)
                        b_dw(ci, bi - 1)
                        b_dw(ci, bi)
                if pre is not None:
                    run_all(pre)
                continue
            for bi in range(nb):
                b_in(ci, b, bi)
            for bi in range(nb):
                b_dw(ci, bi)
        def ln_g():
            (m_t, m_r), (v_t, v_r), (rs_t, rs_r) = RS.items[0], RS.items[1], RS.items[2]
            p1 = (ps[6], PS_RES[6]); p2 = (ps[7], PS_RES[7])
            for (gt, lt, n, v) in blocks:
                sq_t, sq_r = SQB.next()
                act(sq_t[:, 0:4, :n], cat[:, 4:8, lt:lt + n], AF.Square, catB_res(lt, n), [sq_r]); yield
                mm(p1[0][:, :n], [(ones_b[:], cat[:, 4 + ci, lt:lt + n]) for ci in range(4)], catB_res(lt, n) + [R_const], [p1[1]]); yield
                mm(p2[0][:, :n], [(ones_b[:], sq_t[:, ci, :n]) for ci in range(4)], [sq_r, R_const], [p2[1]]); yield
                act(m_t[:, :n], p1[0][:, :n], AF.Copy, [p1[1]], [m_r], scale=1.0 / 512); yield
                tt(v_t[:, :n], m_t[:, :n], m_t[:, :n], ALU.mult, [m_r], [v_r]); yield
                stt(v_t[:, :n], p2[0][:, :n], 1.0 / 512, v_t[:, :n], ALU.mult, ALU.subtract, [p2[1], v_r], [v_r]); yield
                act(rs_t[:, :n], v_t[:, :n], AF.Ln, [v_r, R_const], [rs_r], bias=epsc[:, 0:1], scale=1.0); yield
                act(rs_t[:, :n], rs_t[:, :n], AF.Exp, [rs_r], [rs_r], scale=-0.5); yield
                for ci in range(4):
                    tt(v_t[:, :n], cat[:, 4 + ci, lt:lt + n], m_t[:, :n], ALU.subtract, catB_res(lt, n) + [m_r], [v_r]); yield
                    tt(v_t[:, :n], v_t[:, :n], rs_t[:, :n], ALU.mult, [v_r, rs_r], [v_r]); yield
                    act(cat[:, 4 + ci, lt:lt + n], v_t[:, :n], AF.Silu, [v_r, R_const], catB_res(lt, n),
                        bias=cv_("lnb", ci), scale=cv_("lng", ci)); yield

        ln = ln_g()

        for ci in range(4):
            b = wi % 2; wi += 1
            for j3 in range(3):
                dma("pool", wst[b][:, :, j3 * 128:(j3 + 1) * 128], srcw[:, :, j3 * 512 + ci * 128:j3 * 512 + (ci + 1) * 128],
                    [], [R_wst[b]], d_wst[b])
            for j in range(3):
                ts(dg[:, 31 + j, :], ident_b[:], cv_("scw", j * 4 + ci), None, ALU.mult, None, [R_const], [R_dg])
            for bi, (gt, lt, n, v) in enumerate(blocks):
                pb, pbr = PS.next(); pc, pcr = PS.next(); px, pxr = PS.next()
                mm(pb[:, :n], [(wst[b][:, kc, 0:128], h[:, kc, lt:lt + n]) for kc in range(8)], [R_wst[b]] + h_res(lt, n), [pbr])
                mm(pc[:, :n], [(wst[b][:, kc, 128:256], h[:, kc, lt:lt + n]) for kc in range(8)], [R_wst[b]] + h_res(lt, n), [pcr])
                mm(px[:, :n], [(wst[b][:, kc, 256:384], h[:, kc, lt:lt + n]) for kc in range(8)], [R_wst[b]] + h_res(lt, n), [pxr])
                t_t, t_r = TMP.next()
                act(t_t[:, :n], px[:, :n], AF.Copy, [pxr], [t_r])
                tt(ppad[:, pads3[bi]:pads3[bi] + n], pc[:, :n], t_t[:, :n], ALU.mult, [pcr, t_r], [R_ppad])
                act(gb[:, lt:lt + n], pb[:, :n], AF.Copy, [pbr], [R_gb])
                ln = advance(ln, 3)
            for bi, (gt, lt, n, v) in enumerate(blocks):
                pt, pr = PS.next()
                p0 = pads3[bi] - 1
                mm(pt[:, :n], [(dg[:, 31 + j, :], ppad[:, p0 + j:p0 + j + n]) for j in range(3)], [R_dg, R_ppad], [pr])
                tt(cat[:, ci, lt:lt + n], pt[:, :n], gb[:, lt:lt + n], ALU.mult, [pr, R_gb], cat_res(lt, n))
                ln = advance(ln, 3)
        if ln is not None:
            run_all(ln)
        wout_phase(0, cwout_d, "cbout", blocks, cat, lambda lt, n: cat_res(lt, n) + catB_res(lt, n), h, h_res, wout_at)

    blocksAll = [(i * 512, i * 512, 512, 0) for i in range(4)] + [(2048, 2048, 256, 1), (2304, 2304, 256, 1)]
    conv_layer(blocksAll, NT, [15 + 512 * i for i in range(4)] + [2078, 2349], [1 + 512 * i for i in range(4)] + [2050, 2307])
    S.barrier()
    if stop == "conv":
        S.emit()
        return nc
    GCm = [(2048, 0, 512, 1)]
    GAf = [(0, 0, 512, 0), (512, 512, 512, 0), (2048, 1024, 256, 1)]
    GBf = [(1024, 0, 512, 0), (1536, 512, 512, 0), (2304, 1024, 256, 1)]
    ffn_layer(0, [GAf, GBf], PHASE0)
    S.barrier()
    if stop in ("layer0", "skipconvS"):
        S.emit()
        return nc

    o0 = PHASE0
    KG = sb("KG", [128, 2, NK], BF16, at=o0); o0 += 2 * NK * 2
    VG = sb("VG", [128, 22, 2, 128], BF16, at=o0); o0 += 22 * 2 * 128 * 2
    ckvT = sb("ckvT", [128, 2, NK], BF16, at=o0); o0 += 2 * NK * 2
    KM = [sb("KM0", [96, NK], BF16, at=o0)] * 2; o0 += NK * 2
    ATT0 = o0
    R_KG = S.res("KG"); R_VG = S.res("VG"); R_ckvT = S.res("ckvT"); R_KM = [S.res("KM0")] * 2

    def kidx(gt):
        return 256 + gt

    RHl = [S.res("hscr%d" % i) for i in range(NT // 256)]

    def RH(t0, n):
        return RHl[t0 // 256:(t0 + n + 255) // 256]
    d_hs = S.dsem("d_hs")

    def kside():
        o1 = ATT0
        hg = sb("hgk", [128, 8, 1024], BF16, at=o1); o1 += 16384
        wk = sb("wk", [128, 8, 2, 128], BF16, at=o1); o1 += 8 * 256 * 2
        wkraw = sb("wkraw", [128, 8, 128], BF16, at=o1); o1 += 8 * 128 * 2
        wv = sb("wv", [128, 8, 128], BF16, at=o1); o1 += 8 * 128 * 2
        wckv = sb("wckv", [128, 8, 256], BF16, at=o1); o1 += 8 * 256 * 2
        wkrraw = sb("wkrraw", [128, 8, 128], BF16, at=o1); o1 += 8 * 128 * 2
        cstg = sb("cstg", [128, 2, 768], F32, at=o1); o1 += 2 * 768 * 4
        ostg = sb("ostg", [128, 4, 544], F32, at=o1); o1 += 4 * 544 * 4
        assert o1 <= ARENA1, o1
        hg_res = make_res(4, "hgk")
        R_w = S.res("wkside"); R_cstg = S.res("cstg"); R_ostg = S.res("ostg")
        d_k = S.dsem("d_k"); d_k2 = S.dsem("d_k2"); d_rope = S.dsem("d_ropek"); d_o = S.dsem("d_ko")
        srcw = awin_d.rearrange("(kc p) n -> p kc n", p=128)
        R_wraw = S.res("wkraw")
        dma("pool", wkraw[:], srcw[:, :, 512:640], [], [R_wraw], d_k)
        dma("pool", wv[:], srcw[:, :, 640:768], [], [R_w], d_k)
        dma("pool", wckv[:], srcw[:, :, 1152:1408], [], [R_w], d_k)
        dma("pool", wkrraw[:], srcw[:, :, 1312:1440], [], [R_w], d_k)
        wkr = wkrraw[:, :, 32:128]
        for g in range(2):
            for dup in range(2):
                cp(wk[:, :, g, dup * 64:(dup + 1) * 64], wkraw[:, :, g * 64:(g + 1) * 64], [R_wraw], [R_w])
        S.op("dve", lambda e: e.memset(VG[:], 1.0), writes=[R_VG])
        S.op("dve", lambda e: e.memset(cstg[:], 0.0), writes=[R_cstg])
        for ch in range(2):
            dma("sp", cstg[:, ch, 640:768], cv_d[ch * 128:(ch + 1) * 128, :], [], [R_cstg], d_k2)
        for ch in range(2):
            for g in range(2):
                for dup in range(2):
                    dma("sp", cstg[:, ch, g * 128 + dup * 64:g * 128 + (dup + 1) * 64], ck_d[ch * 128:(ch + 1) * 128, g * 64:(g + 1) * 64],
                        [], [R_cstg], d_k2)
            dma("sp", cstg[:, ch, 256:512], cckv_d[ch * 128:(ch + 1) * 128, :], [], [R_cstg], d_k2)
            dma("sp", cstg[:, ch, 512 + 64:512 + 96], ckr_d[ch * 128:(ch + 1) * 128, :], [], [R_cstg], d_k2)
        for ch in range(2 if "k_cache" not in DEBUG_SKIP else 0):
            cp(VG[:, ch, :, 0:64], cstg[:, ch, 640:768].rearrange("p (g d) -> p g d", g=2), [R_cstg], [R_VG])
            for g in range(2):
                pt, pr = PS.next()
                tr(pt[:, 0:128], cstg[:, ch, g * 128:(g + 1) * 128], ident_f[:], [R_cstg, R_const], [pr])
                cp(KG[:, g, ch * 128:(ch + 1) * 128], pt[:, 0:128], [pr], [R_KG])
            for cc in range(2):
                pt, pr = PS.next()
                tr(pt[:, 0:128], cstg[:, ch, 256 + cc * 128:256 + (cc + 1) * 128], ident_f[:], [R_cstg, R_const], [pr])
                cp(ckvT[:, cc, ch * 128:(ch + 1) * 128], pt[:, 0:128], [pr], [R_ckvT])
            pt, pr = PS.next()
            tr(pt[0:96, 0:128], cstg[:, ch, 512:608], ident_f[:], [R_cstg, R_const], [pr])
            cp(KM[0][64:96, ch * 128:(ch + 1) * 128], pt[64:96, 0:128], [pr], [R_KM[0]])
        BK = Alloc([(ps[i], PS_RES[i]) for i in range(8) if i != 6])
        KT = Alloc([(sb("kt%d" % i, [128, 512], F32, at=o1 + i * 2048), S.res("kt%d" % i)) for i in range(10)])
        o1 += 10 * 2048
        KRS = Alloc([(sb("krs%d" % i, [128, 512], F32, at=o1 + i * 2048), S.res("krs%d" % i)) for i in range(4)])
        o1 += 4 * 2048
        KSQ = Alloc([(sb("ksq%d" % i, [128, 512], BF16, at=o1 + i * 1024), S.res("ksq%d" % i)) for i in range(4)])
        o1 += 4 * 1024
        ropeT = []
        for i in range(2):
            ropeT.append((sb("ropeCk%d" % i, [128, 512], F32, at=o1), sb("ropeSk%d" % i, [128, 512], F32, at=o1 + 2048),
                          sb("rC96k%d" % i, [96, 512], F32, at=o1 + 4096), sb("rS96k%d" % i, [96, 512], F32, at=o1 + 6144),
                          S.res("ropek%d" % i)))
            o1 += 8192
        assert o1 <= ARENA1, o1

        def rstd_chain(pst, rs, n, inv_d):
            act(rs[0][:, :n], pst[0][:, :n], AF.Ln, [pst[1], R_const], [rs[1]], bias=epsc[:, 0:1], scale=inv_d)
            yield
            act(rs[0][:, :n], rs[0][:, :n], AF.Exp, [rs[1]], [rs[1]], scale=-0.5)
            yield

        def chain_k(g, blk, tabs):
            (gt, lt, n, v) = blk
            k0 = kidx(gt); is_s = (v == 0)
            (ropeC, ropeS, rC96, rS96, R_rope) = tabs
            got = yield from acquire((BK, 2), (KT, 3), (KRS, 1), (KSQ, 1))
            (pk, pkr), (p2, p2r) = got[0]
            (kn_t, kn_r), (a_t, a_r), (b_t, b_r) = got[1]
            rs = got[2][0]
            sq_t, sq_r = got[3][0]
            mm(pk[:, :n], [(wk[:, kc, g, :], hg[:, kc, lt:lt + n]) for kc in range(8)], [R_w] + hg_res(lt, n), [pkr]); yield
            act(sq_t[:, :n], pk[:, :n], AF.Square, [pkr], [sq_r]); yield
            mm(p2[:, :n], [(blk64_b[:], sq_t[:, :n])], [sq_r, R_const], [p2r]); yield
            yield from rstd_chain((p2, p2r), rs, n, 1.0 / 64)
            stt(kn_t[:, :n], pk[:, :n], cv_("kn"), rs[0][:, :n], ALU.mult, ALU.mult, [pkr, rs[1], R_const], [kn_r]); yield
            if is_s:
                mm(p2[:, :n], [(P64_sb[:], kn_t[:, :n])], [kn_r, R_const], [p2r]); yield
                tt(a_t[:, :n], kn_t[:, :n], ropeC[:, :n], ALU.mult, [kn_r, R_rope], [a_r]); yield
                tt(b_t[:, :n], p2[:, :n], ropeS[:, :n], ALU.mult, [p2r, R_rope], [b_r]); yield
                tt(KG[:, g, k0:k0 + n], a_t[:, :n], b_t[:, :n], ALU.add, [a_r, b_r], [R_KG]); yield
            else:
                cp(KG[:, g, k0:k0 + n], kn_t[:, :n], [kn_r], [R_KG]); yield
                for t4 in range(n // 128):
                    tr(p2[:, 0:128], kn_t[:, t4 * 128:(t4 + 1) * 128], ident_f[:], [kn_r, R_const], [p2r]); yield
                    slot = ((gt - NS) // 128 + t4)
                    act(ostg[:, slot, g * 64:(g + 1) * 64], p2[:, 0:64], AF.Copy, [p2r], [R_ostg]); yield
            BK.give(got[0]); KT.give(got[1]); KRS.give(got[2]); KSQ.give(got[3])

        def chain_v(blk):
            (gt, lt, n, v) = blk
            k0 = kidx(gt); is_s = (v == 0)
            got = yield from acquire((BK, 1))
            (pt, pr) = got[0][0]
            for t4 in range(n // 128):
                mm(pt[:, 0:128], [(hg[:, kc, lt + t4 * 128:lt + (t4 + 1) * 128], wv[:, kc, :]) for kc in range(8)],
                   [R_w] + hg_res(lt, n), [pr]); yield
                kc_ = (k0 + t4 * 128) // 128
                cp(VG[:, kc_, :, 0:64], pt[:, 0:128].rearrange("p (g d) -> p g d", g=2), [pr], [R_VG]); yield
                if not is_s:
                    slot = ((gt - NS) // 128 + t4)
                    cp(ostg[:, slot, 128:256], pt[:, 0:128], [pr], [R_ostg]); yield
            BK.give(got[0])

        def chain_ckv(blk):
            (gt, lt, n, v) = blk
            k0 = kidx(gt); is_s = (v == 0)
            got = yield from acquire((BK, 3), (KT, 2), (KRS, 1))
            pcs = got[0][0:2]; (p3, p3r) = got[0][2]
            cts = got[1]; rs = got[2][0]
            sq_t, sq_r = SQB.next()
            for cc in range(2):
                mm(pcs[cc][0][:, :n], [(wckv[:, kc, cc * 128:(cc + 1) * 128], hg[:, kc, lt:lt + n]) for kc in range(8)],
                   [R_w] + hg_res(lt, n), [pcs[cc][1]]); yield
                act(sq_t[:, cc, :n], pcs[cc][0][:, :n], AF.Square, [pcs[cc][1]], [sq_r]); yield
            mm(p3[:, :n], [(ones_b[:], sq_t[:, cc, :n]) for cc in range(2)], [sq_r, R_const], [p3r]); yield
            yield from rstd_chain((p3, p3r), rs, n, 1.0 / 256)
            for cc in range(2):
                c_t, c_r = cts[cc]
                stt(c_t[:, :n], pcs[cc][0][:, :n], cv_("kvan", cc), rs[0][:, :n], ALU.mult, ALU.mult,
                    [pcs[cc][1], rs[1], R_const], [c_r]); yield
                act(ckvT[:, cc, k0:k0 + n], c_t[:, :n], AF.Copy, [c_r], [R_ckvT]); yield
                if not is_s:
                    for t4 in range(n // 128):
                        tr(p3[:, 0:128], c_t[:, t4 * 128:(t4 + 1) * 128], ident_f[:], [c_r, R_const], [p3r]); yield
                        slot = ((gt - NS) // 128 + t4)
                        cp(ostg[:, slot, 256 + cc * 128:256 + (cc + 1) * 128], p3[:, 0:128], [p3r], [R_ostg]); yield
            BK.give(got[0]); KT.give(got[1]); KRS.give(got[2])

        def chain_kr(blk, tabs):
            (gt, lt, n, v) = blk
            k0 = kidx(gt); is_s = (v == 0)
            (ropeC, ropeS, rC96, rS96, R_rope) = tabs
            got = yield from acquire((BK, 2), (KT, 3))
            (pk, pkr), (p2, p2r) = got[0]
            (kr_t, kr_r), (a_t, a_r), (b_t, b_r) = got[1]
            mm(pk[0:96, :n], [(wkrraw[:, kc, 32:128], hg[:, kc, lt:lt + n]) for kc in range(8)], [R_w] + hg_res(lt, n), [pkr]); yield
            act(kr_t[0:96, :n], pk[0:96, :n], AF.Copy, [pkr], [kr_r]); yield
            if is_s:
                mm(p2[0:96, :n], [(P96_sb[:], kr_t[0:96, :n])], [kr_r, R_const], [p2r]); yield
                tt(a_t[64:96, :n], kr_t[64:96, :n], rC96[64:96, :n], ALU.mult, [kr_r, R_rope], [a_r]); yield
                tt(b_t[64:96, :n], p2[64:96, :n], rS96[64:96, :n], ALU.mult, [p2r, R_rope], [b_r]); yield
                tt(KM[0][64:96, k0:k0 + n], a_t[64:96, :n], b_t[64:96, :n], ALU.add, [a_r, b_r], [R_KM[0]]); yield
            else:
                cp(KM[0][64:96, k0:k0 + n], kr_t[64:96, :n], [kr_r], [R_KM[0]]); yield
                for t4 in range(n // 128):
                    tr(p2[:, 0:96], kr_t[0:96, t4 * 128:(t4 + 1) * 128], ident_f[0:96, 0:96], [kr_r, R_const], [p2r]); yield
                    slot = ((gt - NS) // 128 + t4)
                    act(ostg[:, slot, 512:544], p2[:, 64:96], AF.Copy, [p2r], [R_ostg]); yield
            BK.give(got[0]); KT.give(got[1])

        kblocks = []
        for bi, (gt, _lt, n, v) in enumerate(GA + GB + GC):
            kblocks.append((gt, (bi % 2) * 512, n, v))
        run_all(prenorm_g(1, 0, [kblocks[0]], hg, hg_res))
        for bi, blk in enumerate(kblocks):
            (gt, lt, n, v) = blk
            dma("sp", h_scr[:, :, gt:gt + n], hg[:, :, lt:lt + n], hg_res(lt, n), RH(gt, n), d_hs)
            tabs = ropeT[bi % 2]
            if v == 0:
                dma("sp", tabs[0][:, :n], C64_d[:, gt:gt + n], [], [tabs[4]], d_rope)
                dma("sp", tabs[1][:, :n], S64_d[:, gt:gt + n], [], [tabs[4]], d_rope)
                dma("sp", tabs[2][:, :n], C96_d[:, gt:gt + n], [], [tabs[4]], d_rope)
                dma("sp", tabs[3][:, :n], S96_d[:, gt:gt + n], [], [tabs[4]], d_rope)
            chains = [chain_k(0, blk, tabs), chain_k(1, blk, tabs), chain_v(blk), chain_ckv(blk), chain_kr(blk, tabs)]
            if bi + 1 < len(kblocks):
                chains.append(prenorm_g(1, 0, [kblocks[bi + 1]], hg, hg_res))
            interleave(chains)

        for slot in range(4 if "k_out" not in DEBUG_SKIP else 0):
            dma("sp", nk_d[slot * 128:(slot + 1) * 128, :], ostg[:, slot, 0:128], [R_ostg], [], d_o)
            dma("sp", nv_d[slot * 128:(slot + 1) * 128, :], ostg[:, slot, 128:256], [R_ostg], [], d_o)
            dma("sp", nckv_d[slot * 128:(slot + 1) * 128, :], ostg[:, slot, 256:512], [R_ostg], [], d_o)
            dma("sp", nkr_d[slot * 128:(slot + 1) * 128, :], ostg[:, slot, 512:544], [R_ostg], [], d_o)

    kside()
    S.barrier()
    if stop == "kside":
        S.emit()
        return nc

    def attn_all():
        o1 = ATT0
        hg = sb("hgq", [128, 8, 1024], BF16, at=o1); o1 += 16384
        cat = sb("cat1", [128, 8, 1024], BF16, at=o1); o1 += 16384
        QG = sb("QG", [128, 8, 1024], BF16, at=o1); o1 += 16384
        qa = sb("qa", [128, 3, 1024], BF16, at=o1); o1 += 6144
        QM = [sb("QM%d" % i, [96, 1024], BF16, at=o1 + i * 2048) for i in range(2)]; o1 += 4096
        VM0_ = sb("VM0", [128, 22, 128], BF16, at=o1); o1 += 5632
        KM1_ = sb("KM1", [96, NK], BF16, at=ATT0)
        VM1_ = sb("VM1", [128, 22, 128], BF16, at=ATT0 + 5632)
        VM = [VM0_, VM1_]
        KMq = [KM[0], KM1_]
        wqb = sb("wqb", [128, 3, 768], BF16, at=o1); o1 += 4608
        wkvb = sb("wkvb", [128, 2, 1024], BF16, at=o1); o1 += 4096
        rC96 = sb("rC96q", [96, 1024], F32, at=o1); o1 += 4096
        rS96 = sb("rS96q", [96, 1024], F32, at=o1); o1 += 4096
        pT = [sb("pT%d" % i, [128, 512], BF16, at=o1 + i * 1024) for i in range(4)]; o1 += 4096
        wout_at = o1
        wq = sb("wq", [128, 8, 512], BF16, at=o1); o1 += 8192
        wqa = sb("wqa", [128, 8, 384], BF16, at=o1); o1 += 6144
        ropeC = sb("ropeCq", [128, 512], F32, at=o1); o1 += 2048
        ropeS = sb("ropeSq", [128, 512], F32, at=o1); o1 += 2048
        assert o1 <= ARENA1, o1
        PT = Rot([(pT[i], S.res("pT%d" % i)) for i in range(4)])
        hg_res = make_res(4, "hgq"); cat_res = make_res(4, "cat1")
        R_QG = make_res(4, "QG"); R_qa = make_res(4, "qa")
        R_QM = [S.res("QM0"), S.res("QM1")]
        R_VMl = [[S.res("VM0")], [S.res("VM1")] + hg_res(0, 1024)]
        R_KMl = [[R_KM[0]], [S.res("KM1")] + hg_res(0, 1024)]
        R_w = S.res("wqside"); R_rope = S.res("ropeq"); R_r96 = S.res("rope96q")
        d_q = S.dsem("d_q"); d_rope = S.dsem("d_ropeq")
        srcw = awin_d.rearrange("(kc p) n -> p kc n", p=128)
        dma("pool", wq[:], srcw[:, :, 0:512], [], [R_w], d_q)
        dma("pool", wqa[:], srcw[:, :, 768:1152], [], [R_w], d_q)
        dma("pool", wqb[:], wqb_d.rearrange("(kc p) n -> p kc n", p=128), [], [R_w], d_q)
        dma("pool", wkvb[:, :, 0:512], wkvb_d.rearrange("(kc p) n -> p kc n", p=128)[:, :, 0:512], [], [R_w], d_q)
        dma("pool", wkvb[:, :, 512:1024], wkvb_d.rearrange("(kc p) n -> p kc n", p=128)[:, :, 512:1024], [], [R_w], d_q)
        BKq = Alloc([(ps[i], PS_RES[i]) for i in range(8)])
        KTq = Alloc(list(TMP.items)[:4])
        KRSq = Alloc(list(RS.items))
        KSQq = Alloc(list(TMPB.items))

        def group(G):
            ng = max(lt + n for (_, lt, n, _) in G)
            is_s = (G[0][3] == 0)
            if is_s:
                g0 = G[0][0]
                dma("sp", rC96[:, :ng], C96_d[:, g0:g0 + ng], [], [R_r96], d_rope)
                dma("sp", rS96[:, :ng], S96_d[:, g0:g0 + ng], [], [R_r96], d_rope)
            S.op("dve", lambda e: e.memset(VM[0][:], 1.0), writes=R_VMl[0])
            S.op("dve", lambda e: e.memset(QG[:], 0.0), writes=R_QG(0, 1024))
            for (gt, lt, n, v) in G:
                dma("sp", hg[:, :, lt:lt + n], h_scr[:, :, gt:gt + n], RH(gt, n), hg_res(lt, n), d_rope)
            def chain_q(qc, blk):
                (gt, lt, n, v) = blk
                got = yield from acquire((BKq, 2), (KTq, 2), (KRSq, 1), (KSQq, 1))
                (pq, pqr), (p2, p2r) = got[0]
                (qn_t, qn_r), (b_t, b_r) = got[1]
                rs = got[2][0]
                sq_t, sq_r = got[3][0]
                mm(pq[:, :n], [(wq[:, kc, qc * 128:(qc + 1) * 128], hg[:, kc, lt:lt + n]) for kc in range(8)], [R_w] + hg_res(lt, n), [pqr]); yield
                act(sq_t[:, :n], pq[:, :n], AF.Square, [pqr], [sq_r]); yield
                mm(p2[:, :n], [(blk64_b[:], sq_t[:, :n])], [sq_r, R_const], [p2r]); yield
                act(rs[0][:, :n], p2[:, :n], AF.Ln, [p2r, R_const], [rs[1]], bias=epsc[:, 0:1], scale=1.0 / 64); yield
                act(rs[0][:, :n], rs[0][:, :n], AF.Exp, [rs[1]], [rs[1]], scale=-0.5); yield
                stt(qn_t[:, :n], pq[:, :n], cv_("qn"), rs[0][:, :n], ALU.mult, ALU.mult, [pqr, rs[1], R_const], [qn_r]); yield
                if v == 0:
                    mm(p2[:, :n], [(P64_sb[:], qn_t[:, :n])], [qn_r, R_const], [p2r]); yield
                    tt(b_t[:, :n], p2[:, :n], ropeS[:, :n], ALU.mult, [p2r, R_rope], [b_r]); yield
                    tt(qn_t[:, :n], qn_t[:, :n], ropeC[:, :n], ALU.mult, [qn_r, R_rope], [qn_r]); yield
                    tt(QG[0:64, 2 * qc, lt:lt + n], qn_t[0:64, :n], b_t[0:64, :n], ALU.add, [qn_r, b_r], R_QG(lt, n)); yield
                    tt(QG[64:128, 2 * qc + 1, lt:lt + n], qn_t[64:128, :n], b_t[64:128, :n], ALU.add, [qn_r, b_r], R_QG(lt, n)); yield
                else:
                    cp(QG[0:64, 2 * qc, lt:lt + n], qn_t[0:64, :n], [qn_r], R_QG(lt, n)); yield
                    cp(QG[64:128, 2 * qc + 1, lt:lt + n], qn_t[64:128, :n], [qn_r], R_QG(lt, n)); yield
                BKq.give(got[0]); KTq.give(got[1]); KRSq.give(got[2]); KSQq.give(got[3])

            def chain_qa(blk):
                (gt, lt, n, v) = blk
                got = yield from acquire((BKq, 4), (KRSq, 1))
                pcs = got[0][0:3]; (p3, p3r) = got[0][3]
                rs = got[1][0]
                sq_t, sq_r = SQB.next()
                for cc in range(3):
                    mm(pcs[cc][0][:, :n], [(wqa[:, kc, cc * 128:(cc + 1) * 128], hg[:, kc, lt:lt + n]) for kc in range(8)],
                       [R_w] + hg_res(lt, n), [pcs[cc][1]]); yield
                    act(sq_t[:, cc, :n], pcs[cc][0][:, :n], AF.Square, [pcs[cc][1]], [sq_r]); yield
                mm(p3[:, :n], [(ones_b[:], sq_t[:, cc, :n]) for cc in range(3)], [sq_r, R_const], [p3r]); yield
                act(rs[0][:, :n], p3[:, :n], AF.Ln, [p3r, R_const], [rs[1]], bias=epsc[:, 0:1], scale=1.0 / 384); yield
                act(rs[0][:, :n], rs[0][:, :n], AF.Exp, [rs[1]], [rs[1]], scale=-0.5); yield
                for cc in range(3):
                    stt(qa[:, cc, lt:lt + n], pcs[cc][0][:, :n], cv_("qan", cc), rs[0][:, :n], ALU.mult, ALU.mult,
                        [pcs[cc][1], rs[1], R_const], R_qa(lt, n)); yield
                BKq.give(got[0]); KRSq.give(got[1])

            for blk in G:
                (gt, lt, n, v) = blk
                if v == 0:
                    dma("sp", ropeC[:, :n], C64_d[:, gt:gt + n], [], [R_rope], d_rope)
                    dma("sp", ropeS[:, :n], S64_d[:, gt:gt + n], [], [R_rope], d_rope)
                interleave([chain_q(0, blk), chain_q(1, blk), chain_qa(blk), chain_q(2, blk), chain_q(3, blk)])


            def keyset(gt, v):
                if v == 0:
                    return list(range(18))
                return [18, 19] if gt < NS + 256 else [20, 21]

            def attend(Kt, K_r, kdim, kbase, Qt, Q_r, Vt, V_r, exp_scale, h, recip='dve', ride=None, pend=None, drain=True):
                odd = h % 2
                olo = 64 if odd else 0
                if pend is None:
                    pend = []

                def finish(po, por, lt, n, hf, olo):
                    l_t, l_r = TMP.next()
                    if recip == 'dve':
                        S.op("dve", lambda e: e.reciprocal(l_t[0:64, :n], po[64:128, :n]), reads=[por], writes=[l_r])
                    else:
                        act(l_t[0:64, :n], po[64:128, :n], AF.Ln, [por], [l_r])
                        act(l_t[0:64, :n], l_t[0:64, :n], AF.Exp, [l_r], [l_r], scale=-1.0)
                    return (po, por, l_t, l_r, olo, lt, n, hf)

                for (gt, lt, n, v) in G:
                    po, por = PSA.next()
                    ks = keyset(gt, v)

                    def pv(i, kc, p_t, p_r, po=po, por=por, n=n, nks=len(ks)):
                        lhs = Vt(kc)
                        S.op("pe", lambda e: e.matmul(po[:, :n], lhs, p_t[:, :n], start=(i == 0), stop=(i == nks - 1)),
                             reads=V_r + [p_r], writes=[por])
                    for i, kc in enumerate(ks):
                        pS, pSr = PS.next()
                        mm(pS[:, :n], [(Kt(kc), Qt(lt, n))], K_r + Q_r(lt, n), [pSr])
                        p_t, p_r = PT.next()
                        act(p_t[:, :n], pS[:, :n], AF.Exp, [pSr], [p_r], scale=exp_scale)
                        pend.append((pv, i, kc, p_t, p_r, (po, por, lt, n, h, olo) if i == len(ks) - 1 else None))
                        if len(pend) > 3:
                            f_, i_, kc_, pt_, pr_, fin = pend.pop(0)
                            f_(i_, kc_, pt_, pr_)
                            if fin is not None:
                                yield finish(*fin)
                        if ride is not None and i % 3 == 2:
                            ride[0] = advance(ride[0], 1)
                while drain and pend:
                    f_, i_, kc_, pt_, pr_, fin = pend.pop(0)
                    f_(i_, kc_, pt_, pr_)
                    if fin is not None:
                        yield finish(*fin)

            pendG = []
            for h in range(8):
                kvh, half, qc = h // 4, h % 2, h // 2
                for (po, por, l_t, l_r, olo, lt, n, hf) in attend(
                        lambda kc, kvh=kvh: KG[:, kvh, kc * 128:(kc + 1) * 128], [R_KG], 64, 0,
                        lambda lt, n, h=h: QG[:, h, lt:lt + n], R_QG,
                        lambda kc, kvh=kvh: VG[:, kc, kvh, :], [R_VG], 0.125, h, pend=pendG, drain=(h == 7)):
                    tt(cat[olo:olo + 64, hf // 2, lt:lt + n], po[0:64, :n], l_t[0:64, :n], ALU.mult,
                       [por, l_r], cat_res(lt, n))
            cp(KMq[1][64:96, :], KMq[0][64:96, :], R_KMl[0], R_KMl[1])
            S.op("dve", lambda e: e.memset(VM[1][:], 1.0), writes=R_VMl[1])
            if is_s:
                kranges = [(k0, min(512, 2304 - k0)) for k0 in range(0, 2304, 512)]
                vchunks = list(range(18))
            else:
                kranges = [(2304, 512)]
                vchunks = [18, 19, 20, 21]

            def prep_g(h):
                b = h % 2
                for (gt, lt, n, v) in G:
                    pq, pqr = PS.next()
                    mm(pq[0:96, :n], [(wqb[:, kc, h * 96:(h + 1) * 96], qa[:, kc, lt:lt + n]) for kc in range(3)], [R_w] + R_qa(lt, n), [pqr])
                    if is_s:
                        q_t, q_r = TMP.next()
                        cp(q_t[0:96, :n], pq[0:96, :n], [pqr], [q_r])
                        pr_, prr = PS.next()
                        mm(pr_[0:96, :n], [(P96_sb[:], q_t[0:96, :n])], [q_r, R_const], [prr])
                        a_t, a_r = TMP.next()
                        tt(a_t[0:96, :n], q_t[0:96, :n], rC96[:, lt:lt + n], ALU.mult, [q_r, R_r96], [a_r])
                        b_t, b_r = TMP.next()
                        tt(b_t[0:96, :n], pr_[0:96, :n], rS96[:, lt:lt + n], ALU.mult, [prr, R_r96], [b_r])
                        tt(QM[b][:, lt:lt + n], a_t[0:96, :n], b_t[0:96, :n], ALU.add, [a_r, b_r], [R_QM[b]])
                    else:
                        act(QM[b][:, lt:lt + n], pq[0:96, :n], AF.Copy, [pqr], [R_QM[b]])
                    yield
                for (k0, kn_) in kranges:
                    pk, pkr = PS.next()
                    mm(pk[0:64, :kn_], [(wkvb[:, cc, h * 128:h * 128 + 64], ckvT[:, cc, k0:k0 + kn_]) for cc in range(2)], [R_w, R_ckvT], [pkr])
                    cp(KMq[b][0:64, k0:k0 + kn_], pk[0:64, :kn_], [pkr], R_KMl[b])
                    yield
                for i0_ in range(0, len(vchunks), 8):
                    grp = vchunks[i0_:i0_ + 8]
                    pv, pvr = PS.next()
                    for j, kc in enumerate(grp):
                        mm(pv[:, j * 64:(j + 1) * 64], [(ckvT[:, cc, kc * 128:(kc + 1) * 128], wkvb[:, cc, h * 128 + 64:h * 128 + 128]) for cc in range(2)],
                           [R_w, R_ckvT], [pvr])
                    cp(VM[b][:, grp[0]:grp[0] + len(grp), 0:64], pv[:, 0:len(grp) * 64].rearrange("p (j d) -> p j d", d=64), [pvr], R_VMl[b])
                    yield

            run_all(prep_g(0))
            pendM = []
            for h in range(8):
                b = h % 2
                nxt = [prep_g(h + 1) if h < 7 else None]
                for (po, por, l_t, l_r, olo, lt, n, hf) in attend(
                        lambda kc, b=b: KMq[b][0:96, kc * 128:(kc + 1) * 128], R_KMl[b], 96, 0,
                        lambda lt, n, b=b: QM[b][0:96, lt:lt + n], lambda lt, n, b=b: [R_QM[b]],
                        lambda kc, b=b: VM[b][:, kc, :], R_VMl[b], 96.0 ** -0.5, h, recip='act', ride=nxt, pend=pendM, drain=(h == 7)):
                    tt(cat[olo:olo + 64, 4 + hf // 2, lt:lt + n], po[0:64, :n], l_t[0:64, :n], ALU.mult,
                       [por, l_r], cat_res(lt, n))
                if nxt[0] is not None:
                    run_all(nxt[0])

            Gm = G if is_s else [(2048, 0, 512, 1)]
            wout_phase(1, awout_d, None, Gm, cat, cat_res, hg, hg_res, None, w_t=QG, w_rl=R_QG(0, 1024))

        for G in (GA, GB, GC):
            group(G)

    attn_all()
    S.barrier()
    ffn_layer(1, [GAf, GBf], PHASE0)
    S.barrier()

    S.emit()
    return nc


_NC_CACHE = {}


def kernel(x_prompt, x_sample, cache_gqa_k, cache_gqa_v, cache_mla_ckv, cache_mla_krope, c, c_ctx,
           ada_w, ada_b, norm_pre, norm_post,
           conv_w_in, conv_sc_w, conv_cf_b_in, conv_cf_dw_w, conv_cf_dw_b, conv_cf_ln_g, conv_cf_ln_b,
           conv_w_out, conv_b_out,
           attn_w_in, attn_q_norm, attn_k_norm, attn_q_a_norm, attn_w_q_b, attn_kv_a_norm, attn_w_kv_b,
           attn_w_out, ffn_w_gate, ffn_w_up, ffn_w_down):
    f = lambda a: np.ascontiguousarray(np.asarray(a, dtype=np.float32))
    if "nc" not in _NC_CACHE:
        _NC_CACHE["nc"] = build_nc()
    nc = _NC_CACHE["nc"]
    C64, S64, P64, C96, S96, P96 = _rope_consts()
    shared = {
        "cctx": f(c_ctx).reshape(8, 128),
        "ada_w": f(ada_w), "ada_b": f(ada_b).reshape(2, 48, 128),
        "norm_pre": f(norm_pre).reshape(32, 128), "norm_post": f(norm_post).reshape(32, 128),
        "conv_w_in": f(conv_w_in)[0], "conv_sc_w": f(conv_sc_w).reshape(12, 128),
        "conv_cf_b_in": f(conv_cf_b_in).reshape(8, 128), "conv_cf_dw_w": f(conv_cf_dw_w).reshape(124, 128),
        "conv_cf_dw_b": f(conv_cf_dw_b).reshape(4, 128), "conv_cf_ln_g": f(conv_cf_ln_g).reshape(4, 128),
        "conv_cf_ln_b": f(conv_cf_ln_b).reshape(4, 128),
        "conv_w_out": f(conv_w_out)[0], "conv_b_out": f(conv_b_out).reshape(8, 128),
        "attn_w_in": f(attn_w_in)[0], "attn_q_norm": f(attn_q_norm).reshape(1, 64), "attn_k_norm": f(attn_k_norm).reshape(1, 64),
        "attn_q_a_norm": f(attn_q_a_norm).reshape(3, 128), "attn_w_q_b": f(attn_w_q_b)[0],
        "attn_kv_a_norm": f(attn_kv_a_norm).reshape(2, 128), "attn_w_kv_b": f(attn_w_kv_b)[0],
        "attn_w_out": f(attn_w_out)[0],
        "ffn_w_gate": f(ffn_w_gate), "ffn_w_up": f(ffn_w_up), "ffn_w_down": f(ffn_w_down),
        "C64": C64, "S64": S64, "P64": P64, "C96": C96, "S96": S96, "P96": P96,
    }
    xp = f(x_prompt); xs = f(x_sample)
    in_maps = []
    for i in range(8):
        m = dict(shared)
        m["xs"] = xs[i]
        m["xp"] = xp[2 * i:2 * i + 2].reshape(NP_, D)
        m["ck"] = f(cache_gqa_k)[i, 0].reshape(256, 128)
        m["cv"] = f(cache_gqa_v)[i, 0].reshape(256, 128)
        m["cckv"] = f(cache_mla_ckv)[i, 0]
        m["ckr"] = f(cache_mla_krope)[i, 0]
        m["crow"] = f(c)[i].reshape(8, 128)
        in_maps.append(m)
    res = run_bass_kernel_spmd(nc, in_maps, core_ids=list(range(8)))
    R = res.results
    y_prompt = np.concatenate([R[i]["yp"].reshape(2, 256, D) for i in range(8)], axis=0)
    y_sample = np.stack([R[i]["ys"] for i in range(8)], axis=0)
    new_k = np.concatenate([R[i]["nk"].reshape(2, 1, 256, 2, 64) for i in range(8)], axis=0)
    new_v = np.concatenate([R[i]["nv"].reshape(2, 1, 256, 2, 64) for i in range(8)], axis=0)
    new_ckv = np.concatenate([R[i]["nckv"].reshape(2, 1, 256, 256) for i in range(8)], axis=0)
    new_kr = np.concatenate([R[i]["nkr"].reshape(2, 1, 256, 32) for i in range(8)], axis=0)
    return (y_prompt.astype(np.float32), y_sample.astype(np.float32), new_k.astype(np.float32),
            new_v.astype(np.float32), new_ckv.astype(np.float32), new_kr.astype(np.float32))
```
